# Optimizing a Trainium2 kernel written in Bass

```python
import math
import jax, jax.numpy as jnp
from jax import lax
import numpy as np

D_MODEL = 1024
BATCH = 8
SEQ = 4096
DEPTH = 1

N_MEM = 256
D_FF = 2816
GM_WIDTH = 512
GM_GROUPS = 4
GM_GROUP_DIM = GM_WIDTH // GM_GROUPS
GM_CHUNK = 128
DIFF_HEADS = 4
DIFF_HEAD_DIM = 64
DIFF_V_DIM = 2 * DIFF_HEAD_DIM
DIFF_Q_WIDTH = DIFF_HEADS * 2 * DIFF_HEAD_DIM
DIFF_V_WIDTH = DIFF_HEADS * DIFF_V_DIM
Q_BLOCK = 128
MEM_HEADS = 4
MEM_HEAD_DIM = 64
MEM_WIDTH = MEM_HEADS * MEM_HEAD_DIM
N_BRANCH = 3
ROPE_THETA = 500000.0
ROPE_DIM = DIFF_HEAD_DIM // 4
DEEPNORM_ALPHA = (2 * DEPTH) ** 0.25
DEEPNORM_BETA = (8 * DEPTH) ** -0.25
LN_EPS = 1e-5
IN_SPLITS = [GM_WIDTH, GM_WIDTH, DIFF_Q_WIDTH, DIFF_Q_WIDTH, DIFF_V_WIDTH, MEM_WIDTH]
IN_WIDTH = sum(IN_SPLITS) + N_BRANCH * D_MODEL

kernel_name = "hybrid_gmlp_diffattn_memxattn_macaron_deepnorm"


def layer_norm(x, g, b):
    xf = x.astype(jnp.float32)
    mu = jnp.mean(xf, axis=-1, keepdims=True)
    var = jnp.mean(jnp.square(xf - mu), axis=-1, keepdims=True)
    y = (xf - mu) * lax.rsqrt(var + LN_EPS) * g.astype(jnp.float32) + b.astype(jnp.float32)
    return y.astype(x.dtype)


def rms_norm(x, g):
    xf = x.astype(jnp.float32)
    y = xf * lax.rsqrt(jnp.mean(jnp.square(xf), axis=-1, keepdims=True) + LN_EPS) * g.astype(jnp.float32)
    return y.astype(x.dtype)


def swiglu(x, w_in, w_out):
    gate, up = jnp.split(x @ w_in, 2, axis=-1)
    return (jax.nn.silu(gate) * up) @ w_out


def rope_tables(positions):
    inv_freq = ROPE_THETA ** (-jnp.arange(0, ROPE_DIM, 2, dtype=jnp.float32) / ROPE_DIM)
    ang = positions.astype(jnp.float32)[..., None] * inv_freq
    return jnp.cos(ang), jnp.sin(ang)


def apply_partial_rope(t, cos, sin):
    c = cos[:, :, None, None, :]
    s = sin[:, :, None, None, :]
    half = ROPE_DIM // 2
    rot = t[..., :ROPE_DIM].astype(jnp.float32)
    x1, x2 = rot[..., :half], rot[..., half:]
    rotated = jnp.concatenate([x1 * c - x2 * s, x2 * c + x1 * s], axis=-1).astype(t.dtype)
    return jnp.concatenate([rotated, t[..., ROPE_DIM:]], axis=-1)


def gmlp_branch(u, v, ln_g, ln_b, w_s, b_s):
    B, S, _ = v.shape
    nc = S // GM_CHUNK
    vn = layer_norm(v, ln_g, ln_b).reshape(B, nc, GM_CHUNK, GM_GROUPS, GM_GROUP_DIM)
    causal = jnp.tril(jnp.ones((GM_CHUNK, GM_CHUNK), dtype=w_s.dtype))
    mixed = jnp.einsum('gts,bcsgd->bctgd', w_s * causal, vn) + b_s.T[:, :, None]
    return u * mixed.reshape(B, S, GM_WIDTH)


def diff_attention(q, k, v, lam, lam_init, norm_g):
    B, S = q.shape[0], q.shape[1]
    nb = S // Q_BLOCK
    scale = DIFF_HEAD_DIM ** -0.5
    q_blocks = jnp.moveaxis(q.reshape(B, nb, Q_BLOCK, DIFF_HEADS, 2, DIFF_HEAD_DIM), 1, 0)
    kpos = jnp.arange(S)

    def one_block(args):
        q_blk, bi = args
        s = jnp.einsum('bqhrd,bkhrd->bhrqk', q_blk, k).astype(jnp.float32) * scale
        qpos = bi * Q_BLOCK + jnp.arange(Q_BLOCK)
        s = jnp.where(kpos[None, :] <= qpos[:, None], s, -jnp.inf)
        p = jax.nn.softmax(s, axis=-1)
        a = p[:, :, 0] - lam * p[:, :, 1]
        return jnp.einsum('bhqk,bkhe->bqhe', a.astype(v.dtype), v)

    o = lax.map(one_block, (q_blocks, jnp.arange(nb)))
    o = jnp.moveaxis(o, 0, 1).reshape(B, S, DIFF_HEADS, DIFF_V_DIM)
    o = rms_norm(o, norm_g) * (1.0 - lam_init)
    return o.reshape(B, S, DIFF_V_WIDTH)


def memory_attention(q, mem, w_kv):
    B, M = mem.shape[0], mem.shape[1]
    k, v = jnp.split(mem @ w_kv, 2, axis=-1)
    k = k.reshape(B, M, MEM_HEADS, MEM_HEAD_DIM)
    v = v.reshape(B, M, MEM_HEADS, MEM_HEAD_DIM)
    s = jnp.einsum('bshd,bmhd->bhsm', q, k).astype(jnp.float32) * (MEM_HEAD_DIM ** -0.5)
    p = jax.nn.softmax(s, axis=-1)
    o = jnp.einsum('bhsm,bmhd->bshd', p.astype(v.dtype), v)
    return o.reshape(q.shape[0], q.shape[1], MEM_WIDTH)


def setup_inputs(seed: int = 0) -> dict:
    key = jax.random.key(seed)
    ks = jax.random.split(key, 40)
    f32 = jnp.float32
    L = DEPTH

    def w(k, shape, fan_in, extra=1.0):
        return jax.random.normal(k, shape, f32) * (fan_in ** -0.5) * extra

    def gain(k, shape):
        return 1.0 + 0.02 * jax.random.normal(k, shape, f32)

    def bias(k, shape):
        return 0.02 * jax.random.normal(k, shape, f32)

    return {
        "x": jax.random.normal(ks[0], (BATCH, SEQ, D_MODEL), f32),
        "mem": jax.random.normal(ks[1], (BATCH, N_MEM, D_MODEL), f32),
        "positions": jnp.broadcast_to(jnp.arange(SEQ, dtype=jnp.int32), (BATCH, SEQ)),
        "ffn1_w_in": w(ks[2], (L, D_MODEL, 2 * D_FF), D_MODEL),
        "ffn1_w_out": w(ks[3], (L, D_FF, D_MODEL), D_FF, DEEPNORM_BETA),
        "ln1_g": gain(ks[4], (L, D_MODEL)),
        "ln1_b": bias(ks[5], (L, D_MODEL)),
        "w_in": w(ks[6], (L, D_MODEL, IN_WIDTH), D_MODEL),
        "gate_b": bias(ks[7], (L, N_BRANCH * D_MODEL)),
        "gm_ln_g": gain(ks[8], (L, GM_WIDTH)),
        "gm_ln_b": bias(ks[9], (L, GM_WIDTH)),
        "gm_w_s": w(ks[10], (L, GM_GROUPS, GM_CHUNK, GM_CHUNK), GM_CHUNK),
        "gm_b_s": gain(ks[11], (L, GM_GROUPS, GM_CHUNK)),
        "lambda_q1": 0.1 * jax.random.normal(ks[12], (L, DIFF_HEAD_DIM), f32),
        "lambda_k1": 0.1 * jax.random.normal(ks[13], (L, DIFF_HEAD_DIM), f32),
        "lambda_q2": 0.1 * jax.random.normal(ks[14], (L, DIFF_HEAD_DIM), f32),
        "lambda_k2": 0.1 * jax.random.normal(ks[15], (L, DIFF_HEAD_DIM), f32),
        "diff_norm_g": gain(ks[16], (L, DIFF_V_DIM)),
        "w_mem_kv": w(ks[17], (L, D_MODEL, 2 * MEM_WIDTH), D_MODEL),
        "w_branch_gm": w(ks[18], (L, GM_WIDTH, D_MODEL), GM_WIDTH),
        "w_branch_diff": w(ks[19], (L, DIFF_V_WIDTH, D_MODEL), DIFF_V_WIDTH),
        "w_branch_mem": w(ks[20], (L, MEM_WIDTH, D_MODEL), MEM_WIDTH),
        "w_o": w(ks[21], (L, D_MODEL, D_MODEL), D_MODEL, DEEPNORM_BETA),
        "ln2_g": gain(ks[22], (L, D_MODEL)),
        "ln2_b": bias(ks[23], (L, D_MODEL)),
        "ffn2_w_in": w(ks[24], (L, D_MODEL, 2 * D_FF), D_MODEL),
        "ffn2_w_out": w(ks[25], (L, D_FF, D_MODEL), D_FF, DEEPNORM_BETA),
        "ln3_g": gain(ks[26], (L, D_MODEL)),
        "ln3_b": bias(ks[27], (L, D_MODEL)),
    }


def reference(x, mem, positions, ffn1_w_in, ffn1_w_out, ln1_g, ln1_b, w_in, gate_b,
              gm_ln_g, gm_ln_b, gm_w_s, gm_b_s, lambda_q1, lambda_k1, lambda_q2, lambda_k2,
              diff_norm_g, w_mem_kv, w_branch_gm, w_branch_diff, w_branch_mem, w_o,
              ln2_g, ln2_b, ffn2_w_in, ffn2_w_out, ln3_g, ln3_b):
    B, S, _ = x.shape
    cos, sin = rope_tables(positions)
    offsets = [int(o) for o in np.cumsum(IN_SPLITS)]
    for i in range(DEPTH):
        lam_init = 0.8 - 0.6 * math.exp(-0.3 * i)
        x = layer_norm(DEEPNORM_ALPHA * x + 0.5 * swiglu(x, ffn1_w_in[i], ffn1_w_out[i]), ln1_g[i], ln1_b[i])

        h = x @ w_in[i]
        u_gm, v_gm, q_d, k_d, v_d, q_m, gate_logits = jnp.split(h, offsets, axis=-1)

        y_gm = gmlp_branch(jax.nn.gelu(u_gm), jax.nn.gelu(v_gm), gm_ln_g[i], gm_ln_b[i], gm_w_s[i], gm_b_s[i])

        q_d = apply_partial_rope(q_d.reshape(B, S, DIFF_HEADS, 2, DIFF_HEAD_DIM), cos, sin)
        k_d = apply_partial_rope(k_d.reshape(B, S, DIFF_HEADS, 2, DIFF_HEAD_DIM), cos, sin)
        v_d = v_d.reshape(B, S, DIFF_HEADS, DIFF_V_DIM)
        lam = (jnp.exp(jnp.sum(lambda_q1[i].astype(jnp.float32) * lambda_k1[i].astype(jnp.float32)))
               - jnp.exp(jnp.sum(lambda_q2[i].astype(jnp.float32) * lambda_k2[i].astype(jnp.float32)))
               + lam_init)
        y_diff = diff_attention(q_d, k_d, v_d, lam, lam_init, diff_norm_g[i])

        y_mem = memory_attention(q_m.reshape(B, S, MEM_HEADS, MEM_HEAD_DIM), mem, w_mem_kv[i])

        g = jax.nn.sigmoid(gate_logits + gate_b[i]).reshape(B, S, N_BRANCH, D_MODEL)
        merged = (g[:, :, 0] * (y_gm @ w_branch_gm[i])
                  + g[:, :, 1] * (y_diff @ w_branch_diff[i])
                  + g[:, :, 2] * (y_mem @ w_branch_mem[i]))
        x = layer_norm(DEEPNORM_ALPHA * x + merged @ w_o[i], ln2_g[i], ln2_b[i])

        x = layer_norm(DEEPNORM_ALPHA * x + 0.5 * swiglu(x, ffn2_w_in[i], ffn2_w_out[i]), ln3_g[i], ln3_b[i])
    return x
```

```python
import math
from contextlib import ExitStack

import numpy as np
import concourse.bass as bass
import concourse.mybir as mybir
from concourse.bass_utils import run_bass_kernel_spmd

F32 = mybir.dt.float32
BF16 = mybir.dt.bfloat16
I32 = mybir.dt.int32
AF = mybir.ActivationFunctionType
ALU = mybir.AluOpType
AX = mybir.AxisListType

P = 128
D = 1024
S = 4096
T = 512
NT_FULL = S // T
KC = 8
DFF = 2816
NJ = DFF // P
NMEM = 256
ALPHA = float(2.0 ** 0.25)
EPS = 1e-5
LAM_INIT = 0.8 - 0.6 * math.exp(0.0)
BLK = 2048
NSLOT = 8
SCHEDULE = True
NSPLIT = 2
CH = 1 << 21
ROPE_THETA = 500000.0
TWO_PI = 2.0 * math.pi
CW1 = 6.28125
CW2 = float(TWO_PI - 6.28125)
PI_LO = 3.1415925


def _block_list():
    bl = [("mk", 2048), ("mv", 2048)]
    bl += [(f"f1_in{j}", 2048) for j in range(NJ)]
    bl += [(f"f1_out{j}", 2048) for j in range(NJ // 2)]
    bl += [("pv0", 2048), ("pv1", 2048), ("pd0", 2048), ("pd1", 2048), ("pu0", 2048), ("pu1", 2048)]
    bl += [(f"pq{h}", 2048) for h in range(4)]
    bl += [(f"pk{h}", 2048) for h in range(4)]
    bl += [("pm", 2048)]
    for nb in range(8):
        bl += [(f"mg{nb}a", 2048), (f"mg{nb}b", 1024), (f"mg{nb}c", 1280)]
    bl += [(f"wo{b}", 2048) for b in range(4)]
    bl += [(f"f2_in{j}", 2048) for j in range(NJ)]
    bl += [(f"f2_out{j}", 2048) for j in range(NJ // 2)]
    return bl


BLOCKS = _block_list()
WB = {}
_off = 0
for _n, _s in BLOCKS:
    WB[_n] = (_off, _s)
    _off += _s
WTOT = _off
NW = WTOT * P
NCHUNK = NW // CH
assert NCHUNK * CH == NW


def _partner_perm():
    idx = np.arange(128)
    d = idx % 64
    base = idx - d
    pd = np.where(d < 8, d + 8, np.where(d < 16, d - 8, d))
    return base + pd


def pack_weights(inp):
    out = np.empty((NW,), dtype=np.float32)

    def put(name, arr):
        off, size = WB[name]
        a = np.ascontiguousarray(arr, dtype=np.float32).reshape(P, size)
        out[off * P:(off + size) * P] = a.reshape(-1)

    def stat_cols(W, cols):
        Wr = W.reshape(KC, P, -1)
        return np.stack([Wr[:, :, c].transpose(1, 0, 2) for c in cols], axis=1)

    def mov_rows(W, r0, nr, c0, nc_):
        return W[r0 * P:(r0 + nr) * P, c0:c0 + nc_].reshape(nr, P, nc_).transpose(1, 0, 2)

    ar = np.arange(128)
    wkv = inp["w_mem_kv"][0]
    put("mk", stat_cols(wkv, [ar, 128 + ar]))
    put("mv", mov_rows(wkv, 0, 8, 256, 256))
    for f, wi, wo in (("f1", inp["ffn1_w_in"][0], inp["ffn1_w_out"][0]),
                      ("f2", inp["ffn2_w_in"][0], inp["ffn2_w_out"][0])):
        for j in range(NJ):
            put(f"{f}_in{j}", stat_cols(wi, [j * 128 + ar, DFF + j * 128 + ar]))
        for jj in range(NJ // 2):
            put(f"{f}_out{jj}", mov_rows(wo, 2 * jj, 2, 0, 1024))
    w_in = inp["w_in"][0]
    for b in range(2):
        put(f"pu{b}", stat_cols(w_in, [(2 * b) * 128 + ar, (2 * b + 1) * 128 + ar]))
        put(f"pv{b}", mov_rows(w_in, 4 * b, 4, 512, 512))
        put(f"pd{b}", mov_rows(w_in, 4 * b, 4, 2048, 512))
    perm = _partner_perm()
    for h in range(4):
        put(f"pq{h}", stat_cols(w_in, [1024 + h * 128 + ar, 1024 + h * 128 + perm]))
        put(f"pk{h}", stat_cols(w_in, [1536 + h * 128 + ar, 1536 + h * 128 + perm]))
    put("pm", stat_cols(w_in, [2560 + ar, 2688 + ar]))
    wbg, wbd, wbm = inp["w_branch_gm"][0], inp["w_branch_diff"][0], inp["w_branch_mem"][0]
    for nb in range(8):
        put(f"mg{nb}a", stat_cols(w_in, [2816 + nb * 128 + ar, 2816 + 1024 + nb * 128 + ar]))
        put(f"mg{nb}b", stat_cols(w_in, [2816 + 2048 + nb * 128 + ar]))
        parts = [wbg[dc * P:(dc + 1) * P, nb * 128:(nb + 1) * 128] for dc in range(4)]
        parts += [wbd[dc * P:(dc + 1) * P, nb * 128:(nb + 1) * 128] for dc in range(4)]
        parts += [wbm[dc * P:(dc + 1) * P, nb * 128:(nb + 1) * 128] for dc in range(2)]
        put(f"mg{nb}c", np.stack(parts, axis=1))
    w_o = inp["w_o"][0]
    for b in range(4):
        put(f"wo{b}", mov_rows(w_o, 2 * b, 2, 0, 1024))
    return out


class Buf:
    __slots__ = ("name", "lw", "rc", "rd", "ap", "lo", "hi")

    def __init__(self, name, ap=None):
        self.name = name
        self.lw = None
        self.rc = set()
        self.rd = set()
        self.ap = ap


class _Op:
    __slots__ = ("fn", "deps", "dma", "inc", "waits", "tag", "ev", "seq", "cost", "grp", "nbytes")

    def __init__(self, fn, deps, dma):
        self.tag = None
        self.ev = None
        self.seq = 0
        self.cost = 0.3
        self.grp = None
        self.nbytes = 0
        self.fn = fn
        self.deps = deps
        self.dma = dma
        self.inc = False
        self.waits = []


class Tracker:
    ENGS = ("pe", "act", "dve", "pool", "sp")
    NOSELF = ("pe", "sp")
    DEFCOST = {"pe": 0.2, "act": 0.5, "dve": 0.3, "pool": 0.4, "sp": 0.05}
    SYNC = 0.15
    TABLE_LOAD = 1.3
    DMA_LAT = 2.0
    DMA_BW = 300e3

    def schedule(self):
        allops = []
        gid = {}
        for eng in self.ENGS:
            for i, op in enumerate(self.ops[eng]):
                gid[(eng, i)] = len(allops)
                allops.append((eng, i, op))
        n = len(allops)
        dgid = {}
        for g, (eng, i, op) in enumerate(allops):
            if op.dma is not None:
                dgid[(op.ev[1], op.ev[2])] = g
        preds = [None] * n
        succ = [[] for _ in range(n)]
        for g, (eng, i, op) in enumerate(allops):
            ps = set()
            for dep in op.deps:
                if dep[0] == "c":
                    ps.add(gid[(dep[1], dep[2])])
                else:
                    ps.add(dgid[(dep[1], dep[2])])
            ps.discard(g)
            preds[g] = ps
            for p_ in ps:
                succ[p_].append(g)
        order = sorted(range(n), key=lambda g: allops[g][2].seq)
        bl = [0.0] * n
        for g in reversed(order):
            op = allops[g][2]
            c = op.cost + (self.DMA_LAT if op.dma is not None else 0.0)
            m = 0.0
            for s_ in succ[g]:
                v = bl[s_] + self.SYNC
                if v > m:
                    m = v
            bl[g] = c + m
        left = [len(preds[g]) for g in range(n)]
        rt = [0.0] * n
        ready = {e: [] for e in self.ENGS}
        for g in order:
            if left[g] == 0:
                ready[allops[g][0]].append(g)
        eng_free = {e: 0.0 for e in self.ENGS}
        dma_free = 0.0
        act_grp = None
        new_order = {e: [] for e in self.ENGS}
        finish = [0.0] * n
        done = 0
        while done < n:
            best = None
            for e in self.ENGS:
                rl = ready[e]
                if not rl:
                    continue
                ef = eng_free[e]
                cand = None
                ckey = None
                for g in rl:
                    op = allops[g][2]
                    st = rt[g] if rt[g] > ef else ef
                    pen = 0.0
                    if e == "act" and op.grp is not None and op.grp != act_grp:
                        pen = self.TABLE_LOAD
                    key = (round((st + pen) * 4), -bl[g], op.seq)
                    if ckey is None or key < ckey:
                        ckey = key
                        cand = (g, st + pen)
                if best is None or cand[1] < best[2] or (cand[1] == best[2] and allops[cand[0]][2].seq < allops[best[1]][2].seq):
                    best = (e, cand[0], cand[1])
            e, g, st = best
            op = allops[g][2]
            ready[e].remove(g)
            if e == "act" and op.grp is not None:
                act_grp = op.grp
            if op.dma is not None:
                eng_free[e] = st + op.cost
                xs_ = st if st > dma_free else dma_free
                dma_free = xs_ + op.nbytes / self.DMA_BW
                finish[g] = dma_free + self.DMA_LAT
            else:
                finish[g] = st + op.cost
                eng_free[e] = finish[g]
            new_order[e].append(g)
            done += 1
            for s_ in succ[g]:
                v = finish[g] + self.SYNC
                if v > rt[s_]:
                    rt[s_] = v
                left[s_] -= 1
                if left[s_] == 0:
                    ready[allops[s_][0]].append(s_)
        self.model_time = max(finish) if n else 0.0
        newidx = {}
        for e in self.ENGS:
            for k, g in enumerate(new_order[e]):
                newidx[(e, allops[g][1])] = k
        for e in self.ENGS:
            self.ops[e] = [allops[g][2] for g in new_order[e]]
        for e in self.ENGS:
            for k, op in enumerate(self.ops[e]):
                nd = set()
                for dep in op.deps:
                    if dep[0] == "c":
                        nd.add(("c", dep[1], newidx[(dep[1], dep[2])]))
                    else:
                        nd.add(dep)
                op.deps = nd
                if op.ev[0] == "c":
                    op.ev = ("c", e, k)

    def __init__(self):
        self.ops = {e: [] for e in self.ENGS}
        self.dma_cnt = {}
        self.tag = ""
        self.seq = 0

    def add(self, eng, fn, reads=(), writes=(), dma=None, extra=(), cost=None, grp=None, nbytes=0):
        deps = set(extra)
        for b in reads:
            if b.lw is not None:
                deps.add(b.lw)
        for b in writes:
            if b.lw is not None:
                deps.add(b.lw)
            deps.update(b.rc)
            deps.update(b.rd)
        idx = len(self.ops[eng])
        if dma is not None:
            cnt = self.dma_cnt.get(dma, 0)
            if cnt > 0:
                deps.add(("d", dma, cnt))
            cnt += 16
            self.dma_cnt[dma] = cnt
            ev = ("d", dma, cnt)
        else:
            ev = ("c", eng, idx)
        op = _Op(fn, deps, dma)
        op.tag = self.tag
        op.ev = ev
        op.seq = self.seq
        self.seq += 1
        op.cost = cost if cost is not None else self.DEFCOST[eng]
        op.grp = grp
        op.nbytes = nbytes
        self.ops[eng].append(op)
        for b in reads:
            if ev[0] == "c":
                b.rc.add(ev)
            else:
                b.rd.add(ev)
        for b in writes:
            b.lw = ev
            b.rc = set()
            b.rd = set()
        return ev

    def resolve(self):
        for eng in self.ENGS:
            wc = {}
            wd = {}
            for op in self.ops[eng]:
                need_c = {}
                need_d = {}
                for dep in op.deps:
                    if dep[0] == "c":
                        _, e2, i2 = dep
                        if e2 == eng and eng in self.NOSELF:
                            continue
                        if i2 <= wc.get(e2, -1):
                            continue
                        if need_c.get(e2, -1) < i2:
                            need_c[e2] = i2
                    else:
                        _, sname, val = dep
                        if val <= wd.get(sname, 0):
                            continue
                        if need_d.get(sname, 0) < val:
                            need_d[sname] = val
                for e2, i2 in need_c.items():
                    op.waits.append(("c", e2, i2))
                    self.ops[e2][i2].inc = True
                    wc[e2] = i2
                for sname, val in need_d.items():
                    op.waits.append(("d", sname, val))
                    wd[sname] = val
        self.cnt = {}
        for eng in self.ENGS:
            c = 0
            arr = []
            for op in self.ops[eng]:
                if op.inc:
                    c += 1
                arr.append(c)
            self.cnt[eng] = arr

    def emit(self, eng, e, sem_c, sem_d):
        for op in self.ops[eng]:
            for w in op.waits:
                if w[0] == "c":
                    e.wait_ge(sem_c[w[1]], self.cnt[w[1]][w[2]])
                else:
                    e.wait_ge(sem_d[w[1]], w[2])
            if op.fn is None:
                continue
            ins = op.fn(e)
            if op.dma is not None:
                ins.then_inc(sem_d[op.dma], 16)
            elif op.inc:
                ins.then_inc(sem_c[eng], 1)


class Scratch:
    def __init__(self, tensor, nwords):
        self.t = tensor
        self.n = nwords
        self.used = []
        self.ghosts = []

    def alloc(self, nwords, name):
        nwords = (nwords + 1) // 2 * 2
        self.used.sort()
        pos = 0
        lo = None
        for (a, b) in self.used:
            if a - pos >= nwords:
                lo = pos
                break
            pos = max(pos, b)
        if lo is None:
            if self.n - pos >= nwords:
                lo = pos
            else:
                raise RuntimeError(f"scratch OOM allocating {name} ({nwords} words); used={self.used}")
        hi = lo + nwords
        self.used.append((lo, hi))
        b = Buf(name)
        b.lo, b.hi = lo, hi
        keep = []
        for g in self.ghosts:
            glo, ghi, lw, rc, rd = g
            if glo < hi and lo < ghi:
                if lw is not None:
                    if lw[0] == "c":
                        b.rc.add(lw)
                    else:
                        b.rd.add(lw)
                b.rc.update(rc)
                b.rd.update(rd)
                if glo >= lo and ghi <= hi:
                    continue
            keep.append(g)
        self.ghosts = keep
        return b

    def free(self, b):
        self.used.remove((b.lo, b.hi))
        self.ghosts.append((b.lo, b.hi, b.lw, set(b.rc), set(b.rd)))

    def f32(self, b, n=None):
        n = (b.hi - b.lo) if n is None else n
        return self.t[:, b.lo:b.lo + n]

    def bf16(self, b, n=None):
        ap = self.t[:, b.lo:b.hi].bitcast(BF16)
        return ap if n is None else ap[:, 0:n]

    def i32(self, b, n=None):
        ap = self.t[:, b.lo:b.hi].bitcast(I32)
        return ap if n is None else ap[:, 0:n]


def build_program(nt=NT_FULL, dbg=None):
    nc = bass.Bass("TRN2", target_bir_lowering=False)
    x_d = nc.dram_tensor("x", [S, D], F32, kind="ExternalInput").ap()
    mem_d = nc.dram_tensor("mem", [NMEM, D], F32, kind="ExternalInput").ap()
    pos_d = nc.dram_tensor("pos", [P, S], I32, kind="ExternalInput").ap()
    wflat_d = nc.dram_tensor("wflat", [NW], F32, kind="ExternalInput").ap()
    lngb_d = nc.dram_tensor("ln_gb", [3, P, 2 * D], F32, kind="ExternalInput").ap()
    lnT_d = nc.dram_tensor("ln_T", [P, 48], F32, kind="ExternalInput").ap()
    gatebT_d = nc.dram_tensor("gate_bT", [P, 24], F32, kind="ExternalInput").ap()
    gmln_d = nc.dram_tensor("gm_ln", [P, 1024], F32, kind="ExternalInput").ap()
    wsT_d = nc.dram_tensor("gm_wsT", [P, 512], F32, kind="ExternalInput").ap()
    bs_d = nc.dram_tensor("gm_bs", [P, 512], F32, kind="ExternalInput").ap()
    lam_d = nc.dram_tensor("lam", [P, 256], F32, kind="ExternalInput").ap()
    dng_d = nc.dram_tensor("dng", [P, 128], F32, kind="ExternalInput").ap()
    cst_d = nc.dram_tensor("consts", [P, 258], F32, kind="ExternalInput").ap()
    out_d = nc.dram_tensor("out", [S, D], F32, kind="ExternalOutput").ap()
    wbf_d = nc.dram_tensor("wbf", [NW], BF16, kind="Internal").ap()
    dbg_d = None
    if dbg:
        dbg_d = {k: nc.dram_tensor("dbg_" + k, list(shp), F32, kind="ExternalOutput").ap() for k, shp in dbg.items()}

    tr = Tracker()
    SCRW = 11264

    with ExitStack() as es:
        def sb(name, shape, dt):
            return es.enter_context(nc.sbuf_tensor(name, shape, dt))

        KT = sb("KT", [P, 4, S], BF16)
        VA = sb("VA", [P, S // P, 4, 130], BF16)
        xres = sb("xres", [P, 4, D], F32)
        xT = sb("xT", [P, KC, T], BF16)
        wring = sb("wring", [P, NSLOT, BLK], BF16)
        lngb = sb("lngb", [P, 2, D], F32)
        cst = sb("cst", [P, 258], F32)
        tri_bf = sb("tri_bf", [P, P], BF16)
        wsT_f = sb("wsT_f", [P, 512], F32)
        wsT_bf = sb("wsT_bf", [P, 4, P], BF16)
        bs_bc = sb("bs_bc", [P, 512], F32)
        gmln = sb("gmln", [P, 1024], F32)
        dngs = sb("dngs", [P, P], F32)
        gatebT = sb("gatebT", [P, 24], F32)
        lam_f = sb("lam_f", [P, 256], F32)
        KmT = sb("KmT", [P, 2, NMEM], BF16)
        Vm = sb("Vm", [P, 2, 4, 66], BF16)
        small = sb("small", [P, 256], F32)
        posi_t = sb("posi", [P, T], I32)
        qz = sb("qz", [P, 2, 4, T], BF16)
        qmz = sb("qmz", [P, 4, T], BF16)
        lnT = sb("lnT", [P, 48], F32)
        ident_bf = sb("ident_bf", [P, P], BF16)
        ki_t = sb("ki", [P, T], I32)
        scr_t = sb("scr", [P, SCRW], F32)
        banks_t = [es.enter_context(nc.psum_tensor(f"bank{i}", [P, 512], F32)) for i in range(8)]

        sem_c = {e: es.enter_context(nc.semaphore("sc_" + e)) for e in Tracker.ENGS}
        dnames = [f"w{s}" for s in range(NSLOT)] + [f"cv{i}" for i in range(NCHUNK)] + \
                 ["x0", "x1", "x2", "x3", "xs0", "xs1", "xs2", "xs3", "o0", "o1", "gb", "pos", "dbg"] + [f"c{i}" for i in range(10)]
        sem_d = {n: es.enter_context(nc.semaphore("sd_" + n)) for n in dnames}

        ident_f = cst[:, 0:128]
        tri_f = cst[:, 128:256]
        freq_ap = cst[:, 256:257]
        sign_ap = cst[:, 257:258]

        banks = [Buf(f"bank{i}", banks_t[i][:, :]) for i in range(8)]
        KTb = [Buf(f"KT{t}") for t in range(NT_FULL)]
        VAb = [Buf(f"VA{t}") for t in range(NT_FULL)]
        xresb = [Buf(f"xres{c}") for c in range(4)]
        xTb = [Buf(f"xT{c}") for c in range(4)]
        slotb = [Buf(f"slot{s}") for s in range(NSLOT)]
        cvb = [Buf(f"cv{i}") for i in range(NCHUNK)]
        lngbb = Buf("lngb")
        cstb = Buf("cst")
        miscb = {n: Buf(n) for n in ("tri_bf", "wsT_f", "wsT_bf", "bs_bc", "gmln", "dngs", "gatebT", "lam_f",
                                     "KmT", "Vm", "neglam", "mhalf", "posi", "ki", "qz", "qmz", "lnT", "identbf")}
        sc = Scratch(scr_t, SCRW)

        small_state = {"i": 0}
        NSM = 10
        smallb = [Buf(f"small{i}") for i in range(NSM)]

        def small_alloc():
            i = small_state["i"] % NSM
            small_state["i"] += 1
            return smallb[i], small[:, 8 + i * 24: 32 + i * 24]

        neglam_ap = small[:, 0:1]
        mhalf_ap = small[:, 1:2]
        lamtmp = small[:, 2:8]

        def nfree(ap):
            n_ = 1
            for d_ in ap.shape[1:]:
                n_ *= d_
            return n_

        DTB = {F32: 4, BF16: 2, I32: 4}

        def MM(out, lhsT, rhs, start, stop, reads, writes, skip=False):
            tr.add("pe", lambda e: e.matmul(out, lhsT=lhsT, rhs=rhs, start=start, stop=stop,
                                            skip_group_check=skip), reads, writes,
                   cost=0.012 + max(nfree(rhs), 64) / 2400.0)

        def TRP(out, in_, reads, writes):
            tr.add("pe", lambda e: e.transpose(out, in_, ident_f), list(reads) + [cstb], writes, cost=0.11)

        def ACT(out, in_, func, reads, writes, bias=None, scale=None):
            kw = {}
            if bias is not None:
                kw["bias"] = bias
            if scale is not None:
                kw["scale"] = scale
            grp = None if func in (AF.Identity, AF.Copy) else func
            tr.add("act", lambda e: e.activation(out=out, in_=in_, func=func, **kw), reads, writes,
                   cost=0.2 + nfree(out) / 1400.0, grp=grp)

        def ecost(eng, out):
            n_ = nfree(out)
            if eng == "pool":
                return 0.35 if n_ <= 8 else 0.3 + n_ / 620.0
            return 0.1 + n_ / 960.0

        def TT(eng, out, in0, in1, op, reads, writes):
            tr.add(eng, lambda e: e.tensor_tensor(out=out, in0=in0, in1=in1, op=op), reads, writes,
                   cost=ecost(eng, out))

        def TS(eng, out, in0, s1, s2, op0, op1, reads, writes):
            if op1 is None:
                tr.add(eng, lambda e: e.tensor_scalar(out=out, in0=in0, scalar1=s1, scalar2=None, op0=op0),
                       reads, writes, cost=ecost(eng, out))
            else:
                tr.add(eng, lambda e: e.tensor_scalar(out=out, in0=in0, scalar1=s1, scalar2=s2, op0=op0, op1=op1),
                       reads, writes, cost=ecost(eng, out))

        def STT(out, in0, scalar, in1, op0, op1, reads, writes):
            tr.add("dve", lambda e: e.scalar_tensor_tensor(out=out, in0=in0, scalar=scalar, in1=in1,
                                                           op0=op0, op1=op1), reads, writes, cost=ecost("dve", out))

        def CP(eng, out, in_, reads, writes):
            if eng == "act":
                ACT(out, in_, AF.Identity, reads, writes)
            else:
                tr.add(eng, lambda e: e.tensor_copy(out=out, in_=in_), reads, writes, cost=ecost(eng, out))

        def DMA(out, in_, sem, reads, writes, eng="sp"):
            nb = nfree(out) * out.shape[0] * DTB.get(out.dtype, 4) + nfree(in_) * (in_.shape[0] if in_.shape[0] > 1 else 1) * 0
            if eng == "pool":
                nb = nb * 3
            return tr.add(eng, lambda e: e.dma_start(out=out, in_=in_), reads, writes, dma=sem,
                          cost=0.1 if eng == "sp" else 1.0, nbytes=nb)

        evac_state = {"i": 0}

        def EVAC(out, in_, reads, writes):
            eng = "act" if evac_state["i"] % 2 == 0 else "dve"
            evac_state["i"] += 1
            CP(eng, out, in_, reads, writes)

        ws_state = {"i": 0}

        cv_state = {"n": 0}
        CV_AHEAD = 2

        def ensure_converted(upto, after=None):
            upto = min(upto, NCHUNK - 1)
            while cv_state["n"] <= upto:
                i = cv_state["n"]
                csrc = wflat_d[i * CH:(i + 1) * CH].rearrange("(a b) -> a b", b=2048)
                cdst = wbf_d[i * CH:(i + 1) * CH].rearrange("(a b) -> a b", b=2048)
                ex = [after] if (after is not None and i >= 1) else []
                tr.add("pool", lambda e, cdst=cdst, csrc=csrc: e.dma_start(out=cdst, in_=csrc), [], [cvb[i]],
                       dma=f"cv{i}", extra=ex, cost=1.0, nbytes=CH * 6)
                cv_state["n"] += 1

        last_wload = {"ev": None}

        def wload(name):
            s = ws_state["i"] % NSLOT
            ws_state["i"] += 1
            off, size = WB[name]
            lo, hi = off * P, (off + size) * P
            c0, c1 = lo // CH, (hi - 1) // CH
            ensure_converted(c1, last_wload["ev"])
            cvs = [cvb[i] for i in range(c0, c1 + 1)]
            src = wbf_d[lo:hi].rearrange("(p f) -> p f", f=size)
            last_wload["ev"] = DMA(wring[:, s, 0:size], src, f"w{s}", cvs, [slotb[s]])
            ensure_converted(c1 + CV_AHEAD, last_wload["ev"])
            return s

        def wview(s, size, pattern, **kw):
            return wring[:, s, 0:size].rearrange(pattern, **kw)

        dbg_cnt = {"i": 0}

        def DUMP(key, ap, reads):
            if dbg_d is not None and key in dbg_d:
                DMA(dbg_d[key], ap, "dbg", reads, [])

        ensure_converted(0)

        DMA(cst[:, :], cst_d[:, :], "c0", [], [cstb])
        DMA(wsT_f[:, :], wsT_d[:, :], "c1", [], [miscb["wsT_f"]])
        DMA(bs_bc[:, :], bs_d[:, :], "c2", [], [miscb["bs_bc"]])
        DMA(gmln[:, :], gmln_d[:, :], "c3", [], [miscb["gmln"]])
        DMA(dngs[:, :], dng_d[:, :], "c4", [], [miscb["dngs"]])
        DMA(gatebT[:, :], gatebT_d[:, :], "c5", [], [miscb["gatebT"]])
        DMA(lam_f[:, :], lam_d[:, :], "c6", [], [miscb["lam_f"]])
        DMA(lnT[:, :], lnT_d[:, :], "c7", [], [miscb["lnT"]])

        CP("dve", tri_bf[:, :], tri_f, [cstb], [miscb["tri_bf"]])
        CP("dve", ident_bf[:, :], ident_f, [cstb], [miscb["identbf"]])
        for g in range(4):
            TT("dve", wsT_bf[:, g, :], wsT_f[:, g * 128:(g + 1) * 128], tri_f, ALU.mult,
               [miscb["wsT_f"], cstb], [miscb["wsT_bf"]])
        TS("dve", dngs[:, :], dngs[:, :], float(1.0 - LAM_INIT), None, ALU.mult, None,
           [miscb["dngs"]], [miscb["dngs"]])
        tr.add("pool", lambda e: e.memset(mhalf_ap, -0.5), [], [miscb["mhalf"]])
        tr.add("pool", lambda e: e.memset(VA[:, :, :, 128:130], 1.0), [], VAb)
        tr.add("pool", lambda e: e.memset(Vm[:, :, :, 64:66], 1.0), [], [miscb["Vm"]])
        tr.add("pool", lambda e: e.memset(qz[:, :, :, :], 0.0), [], [miscb["qz"]], cost=5.0)
        tr.add("pool", lambda e: e.memset(qmz[:, :, :], 0.0), [], [miscb["qmz"]], cost=2.5)
        lb = miscb["lam_f"]
        nb_ = miscb["neglam"]
        ptmp = sc.alloc(128, "lamp")
        pt = sc.f32(ptmp)
        TT("dve", pt[:, 0:64], lam_f[:, 0:64], lam_f[:, 64:128], ALU.mult, [lb], [ptmp])
        TT("dve", pt[:, 64:128], lam_f[:, 128:192], lam_f[:, 192:256], ALU.mult, [lb, ptmp], [ptmp])
        tr.add("dve", lambda e: e.tensor_reduce(out=lamtmp[:, 0:1], in_=pt[:, 0:64], axis=AX.X, op=ALU.add),
               [ptmp], [nb_])
        tr.add("dve", lambda e: e.tensor_reduce(out=lamtmp[:, 1:2], in_=pt[:, 64:128], axis=AX.X, op=ALU.add),
               [ptmp, nb_], [nb_])
        ACT(lamtmp[:, 2:4], lamtmp[:, 0:2], AF.Exp, [nb_], [nb_])
        TT("dve", lamtmp[:, 4:5], lamtmp[:, 2:3], lamtmp[:, 3:4], ALU.subtract, [nb_], [nb_])
        TS("dve", neglam_ap, lamtmp[:, 4:5], -1.0, float(-LAM_INIT), ALU.mult, ALU.add, [nb_], [nb_])
        sc.free(ptmp)

        memtm = sc.alloc(2 * D, "memtm")
        memTb = sc.alloc(KC * NMEM // 2, "memT")
        mt = sc.f32(memtm).rearrange("p (a b) -> p a b", b=D)
        mT = sc.bf16(memTb).rearrange("p (a b) -> p a b", b=NMEM)
        for mc in range(2):
            DMA(mt[:, mc, :], mem_d[mc * P:(mc + 1) * P, :], f"c{8 + mc}", [], [memtm])
        for kc in range(KC):
            bk = banks[kc % 2]
            for mc in range(2):
                TRP(bk.ap[:, mc * P:(mc + 1) * P], mt[:, mc, kc * P:(kc + 1) * P], [memtm], [bk])
            EVAC(mT[:, kc, :], bk.ap[:, 0:NMEM], [bk], [memTb])
        s = wload("mk")
        wv = wview(s, 2048, "p (c k n) -> p c k n", c=2, k=KC)
        for cb in range(2):
            bk = banks[2 + cb]
            for kc in range(KC):
                MM(bk.ap[:, 0:NMEM], wv[:, cb, kc, :], mT[:, kc, :], kc == 0, kc == KC - 1,
                   [slotb[s], memTb], [bk])
            EVAC(KmT[:, cb, :], bk.ap[:, 0:NMEM], [bk], [miscb["KmT"]])
        s = wload("mv")
        wv = wview(s, 2048, "p (k n) -> p k n", k=KC)
        for mc in range(2):
            bk = banks[4 + mc]
            for kc in range(KC):
                MM(bk.ap[:, 0:256], mT[:, kc, mc * P:(mc + 1) * P], wv[:, kc, :], kc == 0, kc == KC - 1,
                   [slotb[s], memTb], [bk])
            EVAC(Vm[:, mc, :, 0:64], bk.ap[:, 0:256].rearrange("p (h e) -> p h e", e=64), [bk], [miscb["Vm"]])
        sc.free(memtm)
        sc.free(memTb)

        def stage_x_load(tile):
            bufs = []
            for c in range(4):
                b = sc.alloc(D, f"xst{c}")
                r0 = tile * T + c * P
                DMA(sc.f32(b), x_d[r0:r0 + P, :], f"xs{c}", [], [b])
                bufs.append(b)
            return bufs

        def stage_x_transpose(bufs):
            for c in range(4):
                src_ap = sc.f32(bufs[c])
                for g in range(2):
                    bk = banks[(2 * c + g) % 8]
                    for k4 in range(4):
                        kc = g * 4 + k4
                        TRP(bk.ap[:, k4 * P:(k4 + 1) * P], src_ap[:, kc * P:(kc + 1) * P], [bufs[c]], [bk])
                    EVAC(xT[:, g * 4:(g + 1) * 4, c * P:(c + 1) * P],
                         bk.ap.rearrange("p (a b) -> p a b", b=P), [bk], [xTb[c]])
                sc.free(bufs[c])

        def load_xres(tile):
            for c in range(4):
                r0 = tile * T + c * P
                DMA(xres[:, c, :], x_d[r0:r0 + P, :], f"x{c}", [], [xresb[c]])

        def transpose_chunk(c, bks, l, zb):
            zbf = sc.bf16(zb)
            for g in range(2):
                bk = bks[g]
                bkb = bk.ap.bitcast(BF16)
                for k4 in range(4):
                    kc = g * 4 + k4
                    tr.add("pe", lambda e, o_=bkb[:, k4 * P:(k4 + 1) * P], i_=zbf[:, kc * P:(kc + 1) * P]:
                           e.transpose(o_, i_, ident_bf[:, :]), [zb, miscb["identbf"]], [bk], cost=0.107)
                for k4 in range(4):
                    kc = g * 4 + k4
                    o_ = xT[:, kc, c * P:(c + 1) * P]
                    i_ = bkb[:, k4 * P:(k4 + 1) * P]
                    ga = lnT[:, l * 16 + kc:l * 16 + kc + 1]
                    be = lnT[:, l * 16 + 8 + kc:l * 16 + 9 + kc]
                    if k4 % 2 == 0:
                        ACT(o_, i_, AF.Identity, [bk, miscb["lnT"]], [xTb[c]], bias=be, scale=ga)
                    else:
                        TS("dve", o_, i_, ga, be, ALU.mult, ALU.add, [bk, miscb["lnT"]], [xTb[c]])
            sc.free(zb)

        def load_gb(l):
            DMA(lngb[:, :, :], lngb_d[l].rearrange("p (a b) -> p a b", b=D), "gb", [], [lngbb])

        def ln_norm(c, fb, bf=False):
            xr = xres[:, c, :]
            xb = xresb[c]
            for hf in range(2):
                STT(xr[:, hf * 512:(hf + 1) * 512], xr[:, hf * 512:(hf + 1) * 512], ALPHA, fb[hf].ap,
                    ALU.mult, ALU.add, [xb, fb[hf]], [xb])
            smb3, sm3 = small_alloc()
            st = sm3[:, 8:20]
            for hf in range(2):
                tr.add("dve", lambda e, hf=hf: e.bn_stats(out=st[:, hf * 6:(hf + 1) * 6],
                                                           in_=xr[:, hf * 512:(hf + 1) * 512]),
                       [xb], [smb3], cost=0.7)
            mv = sm3[:, 0:2]
            tr.add("dve", lambda e: e.bn_aggr(out=mv, in_=st), [smb3], [smb3], cost=0.2)
            ve = sm3[:, 2:3]
            rstd = sm3[:, 3:4]
            nmr = sm3[:, 4:5]
            TS("pool", ve, mv[:, 1:2], float(EPS), None, ALU.add, None, [smb3], [smb3])
            TT("pool", rstd, ve, mhalf_ap, ALU.pow, [smb3, miscb["mhalf"]], [smb3])
            TS("pool", nmr, mv[:, 0:1], rstd, -1.0, ALU.mult, ALU.mult, [smb3], [smb3])
            if not bf:
                ACT(xr, xr, AF.Identity, [xb, smb3], [xb], bias=nmr, scale=rstd)
                return None
            zb = sc.alloc(D // 2, "zb")
            ACT(sc.bf16(zb), xr, AF.Identity, [xb, smb3], [zb], bias=nmr, scale=rstd)
            return zb, smb3, rstd, nmr

        def ln_affine3(c, smb3, rstd, nmr):
            xr = xres[:, c, :]
            xb = xresb[c]
            TS("pool", xr, xr, rstd, nmr, ALU.mult, ALU.add, [xb, smb3], [xb])
            TT("pool", xr, xr, lngb[:, 0, :], ALU.mult, [xb, lngbb], [xb])
            TT("pool", xr, xr, lngb[:, 1, :], ALU.add, [xb, lngbb], [xb])

        def ln_affine(c, dst_ap, dst_bufs):
            xr = xres[:, c, :]
            xb = xresb[c]
            TT("pool", xr, xr, lngb[:, 0, :], ALU.mult, [xb, lngbb], [xb])
            TT("pool", dst_ap, xr, lngb[:, 1, :], ALU.add, [xb, lngbb], dst_bufs)

        GU_BANKS = [(4, 5), (6, 7), (0, 1), (2, 3)]

        def ffn_s1(pre, inter=None):
            tr.tag = pre + ".s1"
            hTb = sc.alloc(NJ * T // 2, "hT")
            hT = sc.bf16(hTb).rearrange("p (j t) -> p j t", t=T)
            sgb = [sc.alloc(T, "sg0"), sc.alloc(T, "sg1")]
            for j in range(NJ):
                s = wload(f"{pre}_in{j}")
                wv = wview(s, 2048, "p (c k n) -> p c k n", c=2, k=KC)
                bg, bu = banks[GU_BANKS[j % 4][0]], banks[GU_BANKS[j % 4][1]]
                if j < NSPLIT:
                    for hf_ in range(2):
                        cs = slice(hf_ * 256, (hf_ + 1) * 256)
                        rd = [slotb[s], xTb[2 * hf_], xTb[2 * hf_ + 1]]
                        for kc in range(KC):
                            MM(bg.ap[:, cs], wv[:, 0, kc, :], xT[:, kc, cs], kc == 0, kc == KC - 1, rd, [bg])
                        for kc in range(KC):
                            MM(bu.ap[:, cs], wv[:, 1, kc, :], xT[:, kc, cs], kc == 0, kc == KC - 1, rd, [bu])
                else:
                    for kc in range(KC):
                        MM(bg.ap, wv[:, 0, kc, :], xT[:, kc, :], kc == 0, kc == KC - 1, [slotb[s]] + xTb, [bg])
                    for kc in range(KC):
                        MM(bu.ap, wv[:, 1, kc, :], xT[:, kc, :], kc == 0, kc == KC - 1, [slotb[s]] + xTb, [bu])
                sg = sgb[j % 2]
                ACT(sc.f32(sg), bg.ap, AF.Silu, [bg], [sg])
                STT(hT[:, j, :], bu.ap, 0.5, sc.f32(sg), ALU.mult, ALU.mult, [bu, sg], [hTb])
                if inter and j in inter:
                    tg = tr.tag
                    inter[j]()
                    tr.tag = tg
            sc.free(sgb[0])
            sc.free(sgb[1])
            return hTb

        PBS = [banks[4:8], banks[0:4]]

        def fbanks(c):
            pb = PBS[c // 2]
            return [pb[(c % 2) * 2], pb[(c % 2) * 2 + 1]]

        def ffn_s3(pre, hTb):
            tr.tag = pre + ".s3"
            hT = sc.bf16(hTb).rearrange("p (j t) -> p j t", t=T)
            for pair in range(2):
                pb = PBS[pair]
                for jj in range(NJ // 2):
                    s = wload(f"{pre}_out{jj}")
                    wv = wview(s, 2048, "p (j n) -> p j n", j=2)
                    for jl in range(2):
                        j = 2 * jj + jl
                        for cl in range(2):
                            c = 2 * pair + cl
                            for hf in range(2):
                                MM(pb[cl * 2 + hf].ap, hT[:, j, c * P:(c + 1) * P], wv[:, jl, hf * 512:(hf + 1) * 512],
                                   j == 0, j == NJ - 1, [slotb[s], hTb], [pb[cl * 2 + hf]])
            sc.free(hTb)

        def ln_stage(tagname, l, after_T=None):
            for c in range(4):
                tr.tag = tagname
                zb, smb3_, rstd_, nmr_ = ln_norm(c, fbanks(c), bf=True)
                transpose_chunk(c, fbanks(c), l, zb)
                ln_affine3(c, smb3_, rstd_, nmr_)
                if after_T:
                    after_T(c)

        out_events = []

        def ln3_chunk(tile, c):
            tg = tr.tag
            tr.tag = "ln3"
            ob = sc.alloc(D, "ostage")
            ln_norm(c, fbanks(c))
            ln_affine(c, sc.f32(ob), [ob])
            r0 = tile * T + c * P
            ev = DMA(out_d[r0:r0 + P, :], sc.f32(ob), f"o{c % 2}", [ob], [])
            out_events.append(ev)
            sc.free(ob)
            tr.tag = tg

        def rope_tables(tile):
            tr.tag = "rope"
            pib = miscb["posi"]
            kib = miscb["ki"]
            DMA(posi_t[:, :], pos_d[:, tile * T:(tile + 1) * T], "pos", [], [pib])
            angb = sc.alloc(T, "ang")
            ang = sc.f32(angb)
            CP("dve", ang, posi_t[:, :], [pib], [angb])
            TS("dve", ang, ang, freq_ap, None, ALU.mult, None, [angb, cstb], [angb])
            outs = []
            for which in range(2):
                a2b = sc.alloc(T, "ang2")
                a2 = sc.f32(a2b)
                if which == 1:
                    TS("dve", a2, ang, float(math.pi / 2), None, ALU.add, None, [angb], [a2b])
                    srcb, srca = a2b, a2
                else:
                    srcb, srca = angb, ang
                kfb = sc.alloc(T, "kf")
                rb = sc.alloc(T, "rr")
                TS("dve", ki_t[:, :], srca, float(1.0 / TWO_PI), None, ALU.mult, None, [srcb], [kib])
                CP("dve", sc.f32(kfb), ki_t[:, :], [kib], [kfb])
                r = sc.f32(rb)
                STT(r, sc.f32(kfb), float(-CW1), srca, ALU.mult, ALU.add, [kfb, srcb], [rb])
                STT(r, sc.f32(kfb), float(-CW2), r, ALU.mult, ALU.add, [kfb, rb], [rb])
                TS("dve", r, r, float(-PI_LO), float(PI_LO), ALU.max, ALU.min, [rb], [rb])
                ob = sc.alloc(T, "ropetab")
                if which == 0:
                    ACT(sc.f32(ob), r, AF.Sin, [rb, cstb], [ob], scale=sign_ap)
                else:
                    ACT(sc.f32(ob), r, AF.Sin, [rb], [ob])
                sc.free(rb)
                sc.free(kfb)
                sc.free(a2b)
                outs.append(ob)
            sc.free(angb)
            return outs[1], outs[0]

        def mixer(tile, Cb, Sb):
            Ct, St = sc.f32(Cb), sc.f32(Sb)
            vnb = sc.alloc(4 * T // 2, "vn")
            vn = sc.bf16(vnb).rearrange("p (c t) -> p c t", t=512)
            sv0, sv1 = wload("pv0"), wload("pv1")
            sd0, sd1 = wload("pd0"), wload("pd1")
            wvv = [wview(sv0, 2048, "p (k n) -> p k n", k=4), wview(sv1, 2048, "p (k n) -> p k n", k=4)]
            wvd = [wview(sd0, 2048, "p (k n) -> p k n", k=4), wview(sd1, 2048, "p (k n) -> p k n", k=4)]

            def v_and_vd(c):
                fb = fbanks(c)
                tr.tag = "v"
                bk = fb[0]
                for kc in range(KC):
                    MM(bk.ap, xT[:, kc, c * P:(c + 1) * P], wvv[kc // 4][:, kc % 4, :], kc == 0, kc == KC - 1,
                       [slotb[sv0], slotb[sv1], xTb[c]], [bk])
                gvb = sc.alloc(T, "gv")
                gv = sc.f32(gvb)
                ACT(gv, bk.ap, AF.Gelu_apprx_tanh, [bk], [gvb])
                smb, sm = small_alloc()
                tr.add("dve", lambda e, sm=sm, gv=gv: e.bn_stats(out=sm[:, 8:14], in_=gv), [gvb], [smb], cost=0.7)
                mv = sm[:, 0:2]
                tr.add("dve", lambda e, mv=mv, sm=sm: e.bn_aggr(out=mv, in_=sm[:, 8:14]), [smb], [smb], cost=0.2)
                ve, rstd, nmr = sm[:, 2:3], sm[:, 3:4], sm[:, 4:5]
                TS("pool", ve, mv[:, 1:2], float(EPS), None, ALU.add, None, [smb], [smb])
                TT("pool", rstd, ve, mhalf_ap, ALU.pow, [smb, miscb["mhalf"]], [smb])
                TS("pool", nmr, mv[:, 0:1], rstd, -1.0, ALU.mult, ALU.mult, [smb], [smb])
                ACT(gv, gv, AF.Identity, [gvb, smb], [gvb], bias=nmr, scale=rstd)
                TT("pool", gv, gv, gmln[:, 0:512], ALU.mult, [gvb, miscb["gmln"]], [gvb])
                TT("pool", vn[:, c, :], gv, gmln[:, 512:1024], ALU.add, [gvb, miscb["gmln"]], [vnb])
                sc.free(gvb)
                tr.tag = "vd"
                bk = fb[1]
                for kc in range(KC):
                    MM(bk.ap, xT[:, kc, c * P:(c + 1) * P], wvd[kc // 4][:, kc % 4, :], kc == 0, kc == KC - 1,
                       [slotb[sd0], slotb[sd1], xTb[c]], [bk])
                EVAC(VA[:, tile * 4 + c, :, 0:128], bk.ap.rearrange("p (h e) -> p h e", e=128), [bk], [VAb[tile]])

            ln_stage("f1.ln", 0, v_and_vd)

            tr.tag = "u"
            uTb = sc.alloc(4 * T, "uT")
            uT = sc.f32(uTb).rearrange("p (c t) -> p c t", t=T)
            bi = 0
            for b in range(2):
                s = wload(f"pu{b}")
                wv = wview(s, 2048, "p (c k n) -> p c k n", c=2, k=KC)
                for cbl in range(2):
                    cb = 2 * b + cbl
                    bk = banks[bi % 8]
                    bi += 1
                    for hf_ in range(2):
                        cs = slice(hf_ * 256, (hf_ + 1) * 256)
                        rd = [slotb[s], xTb[2 * hf_], xTb[2 * hf_ + 1]]
                        for kc in range(KC):
                            MM(bk.ap[:, cs], wv[:, cbl, kc, :], xT[:, kc, cs], kc == 0, kc == KC - 1, rd, [bk])
                    ACT(uT[:, cb, :], bk.ap, AF.Gelu_apprx_tanh, [bk], [uTb])
            tr.tag = "qk"
            for kind in ("q", "k"):
                for h in range(4):
                    s = wload(f"p{kind}{h}")
                    wv = wview(s, 2048, "p (c k n) -> p c k n", c=2, k=KC)
                    bq, bp = banks[bi % 8], banks[(bi + 1) % 8]
                    bi += 2
                    for kc in range(KC):
                        MM(bq.ap, wv[:, 0, kc, :], xT[:, kc, :], kc == 0, kc == KC - 1, [slotb[s]] + xTb, [bq])
                    for kc in range(KC):
                        MM(bp.ap, wv[:, 1, kc, :], xT[:, kc, :], kc == 0, kc == KC - 1, [slotb[s]] + xTb, [bp])
                    t1b, t2b = sc.alloc(T, "rt1"), sc.alloc(T, "rt2")
                    TT("dve", sc.f32(t1b), bq.ap, Ct, ALU.mult, [bq, Cb], [t1b])
                    TT("dve", sc.f32(t2b), bp.ap, St, ALU.mult, [bp, Sb], [t2b])
                    if kind == "q":
                        TT("pool", qz[0:64, 0, h, :], sc.f32(t1b)[0:64, :], sc.f32(t2b)[0:64, :], ALU.add,
                           [t1b, t2b], [miscb["qz"]])
                        TT("pool", qz[64:128, 1, h, :], sc.f32(t1b)[64:128, :], sc.f32(t2b)[64:128, :], ALU.add,
                           [t1b, t2b], [miscb["qz"]])
                    else:
                        TT("pool", KT[:, h, tile * T:(tile + 1) * T], sc.f32(t1b), sc.f32(t2b), ALU.add,
                           [t1b, t2b], [KTb[tile]])
                    sc.free(t1b)
                    sc.free(t2b)
            sc.free(Cb)
            sc.free(Sb)
            tr.tag = "qm"
            s = wload("pm")
            wv = wview(s, 2048, "p (c k n) -> p c k n", c=2, k=KC)
            for cbl in range(2):
                bk = banks[bi % 8]
                bi += 1
                for kc in range(KC):
                    MM(bk.ap, wv[:, cbl, kc, :], xT[:, kc, :], kc == 0, kc == KC - 1, [slotb[s]] + xTb, [bk])
                CP("act", qmz[0:64, 2 * cbl, :], bk.ap[0:64, :], [bk], [miscb["qmz"]])
                CP("dve", qmz[64:128, 2 * cbl + 1, :], bk.ap[64:128, :], [bk], [miscb["qmz"]])
            tr.tag = "gmlp"
            ygmb = sc.alloc(4 * T // 2, "ygmT")
            ygmT = sc.bf16(ygmb).rearrange("p (c t) -> p c t", t=T)
            for c in range(4):
                bk = banks[bi % 8]
                bi += 1
                for g in range(4):
                    MM(bk.ap[:, g * P:(g + 1) * P], vn[:, c, g * P:(g + 1) * P], wsT_bf[:, g, :], True, True,
                       [vnb, miscb["wsT_bf"]], [bk])
                tmb = sc.alloc(T, "gmt")
                tm = sc.f32(tmb)
                TT("dve", tm, bk.ap, bs_bc[:, :], ALU.add, [bk, miscb["bs_bc"]], [tmb])
                TT("pool", ygmT[:, :, c * P:(c + 1) * P], tm.rearrange("p (g t) -> p g t", t=P),
                   uT[:, :, c * P:(c + 1) * P], ALU.mult, [tmb, uTb], [ygmb])
                sc.free(tmb)
            sc.free(vnb)
            sc.free(uTb)
            tr.tag = "attn"
            ydTb = sc.alloc(4 * T // 2, "ydT")
            ydT = sc.bf16(ydTb).rearrange("p (c t) -> p c t", t=T)
            Eb = [[sc.alloc(T // 2, f"E{r}{q}") for q in range(2)] for r in range(2)]
            nk = 4 * tile + 4
            ASETS = [banks[0:3], banks[5:8]]
            SBK = [banks[3], banks[4]]

            def acc(A, r, qs):
                if qs < 3:
                    return A[r], A[r].ap[:, qs * 130: qs * 130 + 129]
                return A[2], A[2].ap[:, r * 130: r * 130 + 129]

            pending_T = []

            def flush_T():
                for (hh, ytb_, TBk) in pending_T:
                    ytm_ = sc.f32(ytb_).rearrange("p (q e) -> p q e", e=P)
                    for qs in range(4):
                        TRP(TBk.ap[:, qs * P:(qs + 1) * P], ytm_[:, qs, :], [ytb_], [TBk])
                    CP("dve", ydT[:, hh, :], TBk.ap, [TBk], [ydTb])
                    sc.free(ytb_)
                pending_T.clear()

            for h in range(4):
                tr.tag = "attn"
                A = ASETS[h % 2]
                started = set()

                def qk(kc, h=h):
                    i = kc - 4 * tile
                    q0 = P * i if i > 0 else 0
                    for r in range(2):
                        bk = SBK[r]
                        MM(bk.ap[:, q0:T], KT[:, h, kc * P:(kc + 1) * P],
                           qz[:, r, h, q0:T], True, True, [KTb[kc // 4], miscb["qz"]], [bk])
                        eb = Eb[r][kc % 2]
                        E = sc.bf16(eb)
                        ACT(E[:, q0:T], bk.ap[:, q0:T], AF.Exp, [bk], [eb], scale=0.125)
                        if i >= 0:
                            TT("pool", E[:, q0:q0 + P], E[:, q0:q0 + P], tri_bf[:, :], ALU.mult,
                               [eb, miscb["tri_bf"]], [eb])

                def av(kc, h=h, started=started, A=A):
                    i = kc - 4 * tile
                    for r in range(2):
                        eb = Eb[r][kc % 2]
                        E = sc.bf16(eb)
                        for qs in range(max(i, 0), 4):
                            ab, aap = acc(A, r, qs)
                            first = ab.name not in started
                            started.add(ab.name)
                            MM(aap, E[:, qs * P:(qs + 1) * P], VA[:, kc, h, 0:129], first, kc == 4 * tile + qs,
                               [eb, VAb[kc // 4]], [ab], skip=True)

                qk(0)
                for kc in range(nk):
                    if kc + 1 < nk:
                        qk(kc + 1)
                    av(kc)
                    if kc == min(2, nk - 1):
                        flush_T()
                        tr.tag = "attn"
                tr.tag = "attn.evac"
                ytb = sc.alloc(4 * P, "ytm")
                ytm = sc.f32(ytb).rearrange("p (q e) -> p q e", e=P)
                ddbs, sms = [], []
                for qs in range(4):
                    b1, a1 = acc(A, 0, qs)
                    b2, a2 = acc(A, 1, qs)
                    smb, sm = small_alloc()
                    tr.add("dve", lambda e, sm=sm, a1=a1: e.reciprocal(out=sm[:, 0:1], in_=a1[:, 128:129]), [b1], [smb], cost=0.15)
                    tr.add("dve", lambda e, sm=sm, a2=a2: e.reciprocal(out=sm[:, 1:2], in_=a2[:, 128:129]), [b2, smb], [smb], cost=0.15)
                    TT("dve", sm[:, 2:3], sm[:, 1:2], neglam_ap, ALU.mult, [smb, miscb["neglam"]], [smb])
                    dtb, ddb = sc.alloc(P, "dt"), sc.alloc(P, "dd")
                    TS("dve", sc.f32(dtb), a1[:, 0:128], sm[:, 0:1], None, ALU.mult, None, [b1, smb], [dtb])
                    STT(sc.f32(ddb), a2[:, 0:128], sm[:, 2:3], sc.f32(dtb), ALU.mult, ALU.add, [b2, smb, dtb], [ddb])
                    dd = sc.f32(ddb)
                    dtf = sc.f32(dtb)
                    TT("pool", dtf, dd, dd, ALU.mult, [ddb, dtb], [dtb])
                    tr.add("dve", lambda e, dtf=dtf, sm=sm: e.tensor_reduce(out=sm[:, 3:4], in_=dtf, axis=AX.X, op=ALU.add),
                           [dtb, smb], [smb], cost=0.25)
                    sc.free(dtb)
                    ddbs.append(ddb)
                    sms.append((smb, sm))
                for qs in range(4):
                    smb, sm = sms[qs]
                    TS("pool", sm[:, 4:5], sm[:, 3:4], float(1.0 / 128), float(EPS), ALU.mult, ALU.add, [smb], [smb])
                    TT("pool", sm[:, 5:6], sm[:, 4:5], mhalf_ap, ALU.pow, [smb, miscb["mhalf"]], [smb])
                for qs in range(4):
                    smb, sm = sms[qs]
                    STT(ytm[:, qs, :], sc.f32(ddbs[qs]), sm[:, 5:6], dngs[:, :], ALU.mult, ALU.mult,
                        [ddbs[qs], smb, miscb["dngs"]], [ytb])
                    sc.free(ddbs[qs])
                pending_T.append((h, ytb, A[2]))
            tr.tag = "mem"
            ymb = sc.alloc(4 * 256, "ymtm")
            ymtm = sc.f32(ymb).rearrange("p (q e) -> p q e", e=256)
            MA = [banks[0], banks[1]]
            units = [(h, mc) for h in range(4) for mc in range(2)]

            def mqk(u):
                h, mc = units[u]
                bk = SBK[u % 2]
                MM(bk.ap, KmT[:, h // 2, mc * P:(mc + 1) * P], qmz[:, h, :], True, True,
                   [miscb["KmT"], miscb["qmz"]], [bk])
                eb = Eb[0][u % 2]
                ACT(sc.bf16(eb), bk.ap, AF.Exp, [bk], [eb], scale=0.125)

            def mav(u):
                h, mc = units[u]
                ab = MA[h % 2]
                eb = Eb[0][u % 2]
                E = sc.bf16(eb)
                for qs in range(4):
                    MM(ab.ap[:, qs * 66:qs * 66 + 65], E[:, qs * P:(qs + 1) * P], Vm[:, mc, h, 0:65],
                       mc == 0 and qs == 0, mc == 1, [eb, miscb["Vm"]], [ab], skip=True)
                if mc == 1:
                    for qs in range(4):
                        smb, sm = small_alloc()
                        aap = ab.ap[:, qs * 66:qs * 66 + 65]
                        tr.add("dve", lambda e, sm=sm, aap=aap: e.reciprocal(out=sm[:, 0:1], in_=aap[:, 64:65]), [ab], [smb], cost=0.15)
                        TS("dve", ymtm[:, qs, h * 64:(h + 1) * 64], aap[:, 0:64], sm[:, 0:1], None, ALU.mult, None,
                           [ab, smb], [ymb])

            mqk(0)
            for u in range(len(units)):
                if u + 1 < len(units):
                    mqk(u + 1)
                mav(u)
                if u == 2:
                    flush_T()
                    tr.tag = "mem"
            for r in range(2):
                for q in range(2):
                    sc.free(Eb[r][q])
            ymTb = sc.alloc(2 * T // 2, "ymT")
            ymT = sc.bf16(ymTb).rearrange("p (c t) -> p c t", t=T)
            for cc in range(2):
                bk = banks[6 + cc]
                for qs in range(4):
                    TRP(bk.ap[:, qs * P:(qs + 1) * P], ymtm[:, qs, cc * P:(cc + 1) * P], [ymb], [bk])
                EVAC(ymT[:, cc, :], bk.ap, [bk], [ymTb])
            sc.free(ymb)
            tr.tag = "merge"
            mgb = sc.alloc(KC * T // 2, "mergedT")
            mgT = sc.bf16(mgb).rearrange("p (c t) -> p c t", t=T)
            gtb = [sc.alloc(T, f"gt{b}") for b in range(3)]
            mtb = [sc.alloc(T, f"mt{b}") for b in range(2)]
            for nb in range(8):
                sa, sb_, scw = wload(f"mg{nb}a"), wload(f"mg{nb}b"), wload(f"mg{nb}c")
                wa = wview(sa, 2048, "p (c k n) -> p c k n", c=2, k=KC)
                wb_ = wview(sb_, 1024, "p (k n) -> p k n", k=KC)
                wc = wview(scw, 1280, "p (i n) -> p i n", i=10)
                for b in range(3):
                    bk = banks[b]
                    for kc in range(KC):
                        lhs = wa[:, b, kc, :] if b < 2 else wb_[:, kc, :]
                        MM(bk.ap, lhs, xT[:, kc, :], kc == 0, kc == KC - 1, [slotb[sa], slotb[sb_]] + xTb, [bk])
                    ACT(sc.f32(gtb[b]), bk.ap, AF.Sigmoid, [bk, miscb["gatebT"]], [gtb[b]],
                        bias=gatebT[:, b * 8 + nb:b * 8 + nb + 1])
                srcs = [(ygmT, ygmb, 4, 0), (ydT, ydTb, 4, 4), (ymT, ymTb, 2, 8)]
                for b, (src_, srcb, ndc, i0) in enumerate(srcs):
                    bk = banks[3 + b]
                    for dc in range(ndc):
                        MM(bk.ap, wc[:, i0 + dc, :], src_[:, dc, :], dc == 0, dc == ndc - 1, [slotb[scw], srcb], [bk])
                m0, m1 = sc.f32(mtb[0]), sc.f32(mtb[1])
                TT("dve", m0, sc.f32(gtb[0]), banks[3].ap, ALU.mult, [gtb[0], banks[3]], [mtb[0]])
                TT("dve", m1, sc.f32(gtb[1]), banks[4].ap, ALU.mult, [gtb[1], banks[4]], [mtb[1]])
                TT("pool", m0, m0, m1, ALU.add, [mtb[0], mtb[1]], [mtb[0]])
                TT("dve", m1, sc.f32(gtb[2]), banks[5].ap, ALU.mult, [gtb[2], banks[5], mtb[1]], [mtb[1]])
                TT("pool", mgT[:, nb, :], m0, m1, ALU.add, [mtb[0], mtb[1]], [mgb])
            for b in gtb + mtb:
                sc.free(b)
            sc.free(ygmb)
            sc.free(ydTb)
            sc.free(ymTb)
            tr.tag = "wo"
            load_gb(1)
            for pair in range(2):
                pb = PBS[pair]
                for b in range(4):
                    s = wload(f"wo{b}")
                    wv = wview(s, 2048, "p (k n) -> p k n", k=2)
                    for kcl in range(2):
                        kc = 2 * b + kcl
                        for cl in range(2):
                            c = 2 * pair + cl
                            for hf in range(2):
                                MM(pb[cl * 2 + hf].ap, mgT[:, kc, c * P:(c + 1) * P], wv[:, kcl, hf * 512:(hf + 1) * 512],
                                   kc == 0, kc == KC - 1, [slotb[s], mgb], [pb[cl * 2 + hf]])
            sc.free(mgb)
            ln_stage("ln2", 1)

        tr.tag = "xT"
        xs = stage_x_load(0)
        stage_x_transpose(xs)
        pending = None
        for tile in range(nt):
            inter = None
            if pending is not None:
                pt_ = pending
                ln3_chunk(pt_, 0)
                ln3_chunk(pt_, 1)
                inter = {1: (lambda pt_=pt_: ln3_chunk(pt_, 2)), 2: (lambda pt_=pt_: ln3_chunk(pt_, 3))}
            hTb = ffn_s1("f1", inter)
            load_xres(tile)
            Cb, Sb = rope_tables(tile)
            load_gb(0)
            ffn_s3("f1", hTb)
            if dbg_d is not None and tile == 0:
                DUMP("x1", xres[:, :, :], xresb)
            mixer(tile, Cb, Sb)
            if dbg_d is not None and tile == 0:
                DUMP("x2", xres[:, :, :], xresb)
            xs_box = {}
            inter = None
            if tile + 1 < nt:
                inter = {12: (lambda t2=tile + 1: xs_box.__setitem__("xs", stage_x_load(t2)))}
            hTb = ffn_s1("f2", inter)
            if tile + 1 < nt:
                tr.tag = "xT"
                stage_x_transpose(xs_box["xs"])
            load_gb(2)
            ffn_s3("f2", hTb)
            pending = tile
        for c in range(4):
            ln3_chunk(pending, c)

        tr.add("sp", None, extra=out_events)
        if dbg_d is not None:
            tr.add("sp", None, extra=[("d", "dbg", tr.dma_cnt.get("dbg", 0))] if tr.dma_cnt.get("dbg", 0) else [])

        if SCHEDULE:
            tr.schedule()
        tr.resolve()

        with nc.Block() as block:
            @block.tensor
            def _(e):
                tr.emit("pe", e, sem_c, sem_d)

            @block.scalar
            def _(e):
                tr.emit("act", e, sem_c, sem_d)

            @block.vector
            def _(e):
                tr.emit("dve", e, sem_c, sem_d)

            @block.gpsimd
            def _(e):
                tr.emit("pool", e, sem_c, sem_d)

            @block.sync
            def _(e):
                tr.emit("sp", e, sem_c, sem_d)

    nc._tr = tr
    return nc


def host_inputs(inp):
    f32 = np.float32
    wflat = pack_weights(inp)
    ln_gb = np.stack([inp["ln1_g"][0], inp["ln1_b"][0], inp["ln2_g"][0], inp["ln2_b"][0],
                      inp["ln3_g"][0], inp["ln3_b"][0]]).astype(f32)
    gate_bT = np.ascontiguousarray(inp["gate_b"][0].reshape(24, P).T).astype(f32)
    gm_ln = np.concatenate([inp["gm_ln_g"][0], inp["gm_ln_b"][0]]).reshape(1, 1024).astype(f32)
    gm_wsT = np.ascontiguousarray(inp["gm_w_s"][0].transpose(2, 0, 1)).reshape(P, 512).astype(f32)
    gm_bs = inp["gm_b_s"][0].reshape(1, 512).astype(f32)
    lam = np.concatenate([inp["lambda_q1"][0], inp["lambda_k1"][0], inp["lambda_q2"][0],
                          inp["lambda_k2"][0]]).reshape(1, 256).astype(f32)
    dng = inp["diff_norm_g"][0].reshape(1, 128).astype(f32)
    consts = np.zeros((P, 258), dtype=f32)
    consts[:, 0:128] = np.eye(P, dtype=f32)
    consts[:, 128:256] = np.triu(np.ones((P, P), dtype=f32))
    inv_freq = ROPE_THETA ** (-np.arange(0, 16, 2, dtype=np.float64) / 16.0)
    d = np.arange(P) % 64
    consts[:, 256] = np.where(d < 16, inv_freq[d % 8], 0.0).astype(f32)
    consts[:, 257] = np.where(d < 8, -1.0, np.where(d < 16, 1.0, 0.0)).astype(f32)
    bc = lambda a: np.ascontiguousarray(np.broadcast_to(a.reshape(1, -1), (P, a.size))).astype(f32)
    ln_bc = np.stack([np.concatenate([bc(ln_gb[2 * l]), bc(ln_gb[2 * l + 1])], axis=1) for l in range(3)])
    ln_T = np.zeros((P, 48), dtype=f32)
    for l in range(3):
        ln_T[:, l * 16:l * 16 + 8] = ln_gb[2 * l].reshape(8, P).T
        ln_T[:, l * 16 + 8:l * 16 + 16] = ln_gb[2 * l + 1].reshape(8, P).T
    shared = dict(wflat=wflat, ln_gb=np.ascontiguousarray(ln_bc), ln_T=ln_T, gate_bT=gate_bT, gm_ln=bc(gm_ln),
                  gm_wsT=gm_wsT, gm_bs=bc(gm_bs), lam=bc(lam), dng=bc(dng), consts=consts)
    return shared


def kernel(**inputs):
    inp = {k: np.asarray(v) for k, v in inputs.items()}
    shared = host_inputs(inp)
    x = np.ascontiguousarray(inp["x"], dtype=np.float32)
    mem = np.ascontiguousarray(inp["mem"], dtype=np.float32)
    pos = np.ascontiguousarray(inp["positions"], dtype=np.int32)
    nc = build_program()
    in_maps = []
    for b in range(8):
        m = dict(shared)
        m["x"] = x[b]
        m["mem"] = mem[b]
        m["pos"] = np.ascontiguousarray(np.broadcast_to(pos[b].reshape(1, S), (P, S)))
        in_maps.append(m)
    res = run_bass_kernel_spmd(nc, in_maps, core_ids=list(range(8)))
    out = np.stack([np.asarray(res.results[b]["out"], dtype=np.float32).reshape(S, D) for b in range(8)], axis=0)
    return out
```

```python
import math
from contextlib import ExitStack

import numpy as np
import concourse.bass as bass
import concourse.mybir as mybir
from concourse.bass_utils import run_bass_kernel_spmd

F32 = mybir.dt.float32
BF16 = mybir.dt.bfloat16
I32 = mybir.dt.int32
AF = mybir.ActivationFunctionType
ALU = mybir.AluOpType
AX = mybir.AxisListType

P = 128
D = 1024
S = 4096
T = 512
NT_FULL = S // T
KC = 8
DFF = 2816
NJ = DFF // P
NMEM = 256
ALPHA = float(2.0 ** 0.25)
EPS = 1e-5
LAM_INIT = 0.8 - 0.6 * math.exp(0.0)
BLK = 2048
NSLOT = 8
SCHEDULE = True
NSPLIT = 2
CH = 1 << 21
ROPE_THETA = 500000.0
TWO_PI = 2.0 * math.pi
CW1 = 6.28125
CW2 = float(TWO_PI - 6.28125)
PI_LO = 3.1415925


def _block_list():
    bl = [("mk", 2048), ("mv", 2048)]
    bl += [(f"f1_in{j}", 2048) for j in range(NJ)]
    bl += [(f"f1_out{j}", 2048) for j in range(NJ // 2)]
    bl += [("pv0", 2048), ("pv1", 2048), ("pd0", 2048), ("pd1", 2048), ("pu0", 2048), ("pu1", 2048)]
    bl += [(f"pq{h}", 2048) for h in range(4)]
    bl += [(f"pk{h}", 2048) for h in range(4)]
    bl += [("pm", 2048)]
    for nb in range(8):
        bl += [(f"mg{nb}a", 2048), (f"mg{nb}b", 1024), (f"mg{nb}c", 1280)]
    bl += [(f"wo{b}", 2048) for b in range(4)]
    bl += [(f"f2_in{j}", 2048) for j in range(NJ)]
    bl += [(f"f2_out{j}", 2048) for j in range(NJ // 2)]
    return bl


BLOCKS = _block_list()
WB = {}
_off = 0
for _n, _s in BLOCKS:
    WB[_n] = (_off, _s)
    _off += _s
WTOT = _off
NW = WTOT * P
NCHUNK = NW // CH
assert NCHUNK * CH == NW


def _partner_perm():
    idx = np.arange(128)
    d = idx % 64
    base = idx - d
    pd = np.where(d < 8, d + 8, np.where(d < 16, d - 8, d))
    return base + pd


def pack_weights(inp):
    out = np.empty((NW,), dtype=np.float32)

    def put(name, arr):
        off, size = WB[name]
        a = np.ascontiguousarray(arr, dtype=np.float32).reshape(P, size)
        out[off * P:(off + size) * P] = a.reshape(-1)

    def stat_cols(W, cols):
        Wr = W.reshape(KC, P, -1)
        return np.stack([Wr[:, :, c].transpose(1, 0, 2) for c in cols], axis=1)

    def mov_rows(W, r0, nr, c0, nc_):
        return W[r0 * P:(r0 + nr) * P, c0:c0 + nc_].reshape(nr, P, nc_).transpose(1, 0, 2)

    ar = np.arange(128)
    wkv = inp["w_mem_kv"][0]
    put("mk", stat_cols(wkv, [ar, 128 + ar]))
    put("mv", mov_rows(wkv, 0, 8, 256, 256))
    for f, wi, wo in (("f1", inp["ffn1_w_in"][0], inp["ffn1_w_out"][0]),
                      ("f2", inp["ffn2_w_in"][0], inp["ffn2_w_out"][0])):
        for j in range(NJ):
            put(f"{f}_in{j}", stat_cols(wi, [j * 128 + ar, DFF + j * 128 + ar]))
        for jj in range(NJ // 2):
            put(f"{f}_out{jj}", mov_rows(wo, 2 * jj, 2, 0, 1024))
    w_in = inp["w_in"][0]
    for b in range(2):
        put(f"pu{b}", stat_cols(w_in, [(2 * b) * 128 + ar, (2 * b + 1) * 128 + ar]))
        put(f"pv{b}", mov_rows(w_in, 4 * b, 4, 512, 512))
        put(f"pd{b}", mov_rows(w_in, 4 * b, 4, 2048, 512))
    perm = _partner_perm()
    for h in range(4):
        put(f"pq{h}", stat_cols(w_in, [1024 + h * 128 + ar, 1024 + h * 128 + perm]))
        put(f"pk{h}", stat_cols(w_in, [1536 + h * 128 + ar, 1536 + h * 128 + perm]))
    put("pm", stat_cols(w_in, [2560 + ar, 2688 + ar]))
    wbg, wbd, wbm = inp["w_branch_gm"][0], inp["w_branch_diff"][0], inp["w_branch_mem"][0]
    for nb in range(8):
        put(f"mg{nb}a", stat_cols(w_in, [2816 + nb * 128 + ar, 2816 + 1024 + nb * 128 + ar]))
        put(f"mg{nb}b", stat_cols(w_in, [2816 + 2048 + nb * 128 + ar]))
        parts = [wbg[dc * P:(dc + 1) * P, nb * 128:(nb + 1) * 128] for dc in range(4)]
        parts += [wbd[dc * P:(dc + 1) * P, nb * 128:(nb + 1) * 128] for dc in range(4)]
        parts += [wbm[dc * P:(dc + 1) * P, nb * 128:(nb + 1) * 128] for dc in range(2)]
        put(f"mg{nb}c", np.stack(parts, axis=1))
    w_o = inp["w_o"][0]
    for b in range(4):
        put(f"wo{b}", mov_rows(w_o, 2 * b, 2, 0, 1024))
    return out


class Buf:
    __slots__ = ("name", "lw", "rc", "rd", "ap", "lo", "hi")

    def __init__(self, name, ap=None):
        self.name = name
        self.lw = None
        self.rc = set()
        self.rd = set()
        self.ap = ap


class _Op:
    __slots__ = ("fn", "deps", "dma", "inc", "waits", "tag", "ev", "seq", "cost", "grp", "nbytes")

    def __init__(self, fn, deps, dma):
        self.tag = None
        self.ev = None
        self.seq = 0
        self.cost = 0.3
        self.grp = None
        self.nbytes = 0
        self.fn = fn
        self.deps = deps
        self.dma = dma
        self.inc = False
        self.waits = []


class Tracker:
    ENGS = ("pe", "act", "dve", "pool", "sp")
    NOSELF = ("pe", "sp")
    DEFCOST = {"pe": 0.2, "act": 0.5, "dve": 0.3, "pool": 0.4, "sp": 0.05}
    SYNC = 0.15
    TABLE_LOAD = 1.3
    DMA_LAT = 2.0
    DMA_BW = 300e3

    def schedule(self):
        allops = []
        gid = {}
        for eng in self.ENGS:
            for i, op in enumerate(self.ops[eng]):
                gid[(eng, i)] = len(allops)
                allops.append((eng, i, op))
        n = len(allops)
        dgid = {}
        for g, (eng, i, op) in enumerate(allops):
            if op.dma is not None:
                dgid[(op.ev[1], op.ev[2])] = g
        preds = [None] * n
        succ = [[] for _ in range(n)]
        for g, (eng, i, op) in enumerate(allops):
            ps = set()
            for dep in op.deps:
                if dep[0] == "c":
                    ps.add(gid[(dep[1], dep[2])])
                else:
                    ps.add(dgid[(dep[1], dep[2])])
            ps.discard(g)
            preds[g] = ps
            for p_ in ps:
                succ[p_].append(g)
        order = sorted(range(n), key=lambda g: allops[g][2].seq)
        bl = [0.0] * n
        for g in reversed(order):
            op = allops[g][2]
            c = op.cost + (self.DMA_LAT if op.dma is not None else 0.0)
            m = 0.0
            for s_ in succ[g]:
                v = bl[s_] + self.SYNC
                if v > m:
                    m = v
            bl[g] = c + m
        left = [len(preds[g]) for g in range(n)]
        rt = [0.0] * n
        ready = {e: [] for e in self.ENGS}
        for g in order:
            if left[g] == 0:
                ready[allops[g][0]].append(g)
        eng_free = {e: 0.0 for e in self.ENGS}
        dma_free = 0.0
        act_grp = None
        new_order = {e: [] for e in self.ENGS}
        finish = [0.0] * n
        done = 0
        while done < n:
            best = None
            for e in self.ENGS:
                rl = ready[e]
                if not rl:
                    continue
                ef = eng_free[e]
                cand = None
                ckey = None
                for g in rl:
                    op = allops[g][2]
                    st = rt[g] if rt[g] > ef else ef
                    pen = 0.0
                    if e == "act" and op.grp is not None and op.grp != act_grp:
                        pen = self.TABLE_LOAD
                    key = (round((st + pen) * 8), -bl[g], op.seq)
                    if ckey is None or key < ckey:
                        ckey = key
                        cand = (g, st + pen)
                if best is None or cand[1] < best[2] or (cand[1] == best[2] and allops[cand[0]][2].seq < allops[best[1]][2].seq):
                    best = (e, cand[0], cand[1])
            e, g, st = best
            op = allops[g][2]
            ready[e].remove(g)
            if e == "act" and op.grp is not None:
                act_grp = op.grp
            if op.dma is not None:
                eng_free[e] = st + op.cost
                xs_ = st if st > dma_free else dma_free
                dma_free = xs_ + op.nbytes / self.DMA_BW
                finish[g] = dma_free + self.DMA_LAT
            else:
                finish[g] = st + op.cost
                eng_free[e] = finish[g]
            new_order[e].append(g)
            done += 1
            for s_ in succ[g]:
                v = finish[g] + self.SYNC
                if v > rt[s_]:
                    rt[s_] = v
                left[s_] -= 1
                if left[s_] == 0:
                    ready[allops[s_][0]].append(s_)
        self.model_time = max(finish) if n else 0.0
        newidx = {}
        for e in self.ENGS:
            for k, g in enumerate(new_order[e]):
                newidx[(e, allops[g][1])] = k
        for e in self.ENGS:
            self.ops[e] = [allops[g][2] for g in new_order[e]]
        for e in self.ENGS:
            for k, op in enumerate(self.ops[e]):
                nd = set()
                for dep in op.deps:
                    if dep[0] == "c":
                        nd.add(("c", dep[1], newidx[(dep[1], dep[2])]))
                    else:
                        nd.add(dep)
                op.deps = nd
                if op.ev[0] == "c":
                    op.ev = ("c", e, k)

    def __init__(self):
        self.ops = {e: [] for e in self.ENGS}
        self.dma_cnt = {}
        self.tag = ""
        self.seq = 0

    def add(self, eng, fn, reads=(), writes=(), dma=None, extra=(), cost=None, grp=None, nbytes=0):
        deps = set(extra)
        for b in reads:
            if b.lw is not None:
                deps.add(b.lw)
        for b in writes:
            if b.lw is not None:
                deps.add(b.lw)
            deps.update(b.rc)
            deps.update(b.rd)
        idx = len(self.ops[eng])
        if dma is not None:
            cnt = self.dma_cnt.get(dma, 0)
            if cnt > 0:
                deps.add(("d", dma, cnt))
            cnt += 16
            self.dma_cnt[dma] = cnt
            ev = ("d", dma, cnt)
        else:
            ev = ("c", eng, idx)
        op = _Op(fn, deps, dma)
        op.tag = self.tag
        op.ev = ev
        op.seq = self.seq
        self.seq += 1
        op.cost = cost if cost is not None else self.DEFCOST[eng]
        op.grp = grp
        op.nbytes = nbytes
        self.ops[eng].append(op)
        for b in reads:
            if ev[0] == "c":
                b.rc.add(ev)
            else:
                b.rd.add(ev)
        for b in writes:
            b.lw = ev
            b.rc = set()
            b.rd = set()
        return ev

    def resolve(self):
        for eng in self.ENGS:
            wc = {}
            wd = {}
            for op in self.ops[eng]:
                need_c = {}
                need_d = {}
                for dep in op.deps:
                    if dep[0] == "c":
                        _, e2, i2 = dep
                        if e2 == eng and eng in self.NOSELF:
                            continue
                        if i2 <= wc.get(e2, -1):
                            continue
                        if need_c.get(e2, -1) < i2:
                            need_c[e2] = i2
                    else:
                        _, sname, val = dep
                        if val <= wd.get(sname, 0):
                            continue
                        if need_d.get(sname, 0) < val:
                            need_d[sname] = val
                for e2, i2 in need_c.items():
                    op.waits.append(("c", e2, i2))
                    self.ops[e2][i2].inc = True
                    wc[e2] = i2
                for sname, val in need_d.items():
                    op.waits.append(("d", sname, val))
                    wd[sname] = val
        self.cnt = {}
        for eng in self.ENGS:
            c = 0
            arr = []
            for op in self.ops[eng]:
                if op.inc:
                    c += 1
                arr.append(c)
            self.cnt[eng] = arr

    def emit(self, eng, e, sem_c, sem_d):
        for op in self.ops[eng]:
            for w in op.waits:
                if w[0] == "c":
                    e.wait_ge(sem_c[w[1]], self.cnt[w[1]][w[2]])
                else:
                    e.wait_ge(sem_d[w[1]], w[2])
            if op.fn is None:
                continue
            ins = op.fn(e)
            if op.dma is not None:
                ins.then_inc(sem_d[op.dma], 16)
            elif op.inc:
                ins.then_inc(sem_c[eng], 1)


class Scratch:
    def __init__(self, tensor, nwords):
        self.t = tensor
        self.n = nwords
        self.used = []
        self.ghosts = []

    def alloc(self, nwords, name):
        nwords = (nwords + 1) // 2 * 2
        self.used.sort()
        pos = 0
        lo = None
        for (a, b) in self.used:
            if a - pos >= nwords:
                lo = pos
                break
            pos = max(pos, b)
        if lo is None:
            if self.n - pos >= nwords:
                lo = pos
            else:
                raise RuntimeError(f"scratch OOM allocating {name} ({nwords} words); used={self.used}")
        hi = lo + nwords
        self.used.append((lo, hi))
        b = Buf(name)
        b.lo, b.hi = lo, hi
        keep = []
        for g in self.ghosts:
            glo, ghi, lw, rc, rd = g
            if glo < hi and lo < ghi:
                if lw is not None:
                    if lw[0] == "c":
                        b.rc.add(lw)
                    else:
                        b.rd.add(lw)
                b.rc.update(rc)
                b.rd.update(rd)
                if glo >= lo and ghi <= hi:
                    continue
            keep.append(g)
        self.ghosts = keep
        return b

    def free(self, b):
        self.used.remove((b.lo, b.hi))
        self.ghosts.append((b.lo, b.hi, b.lw, set(b.rc), set(b.rd)))

    def f32(self, b, n=None):
        n = (b.hi - b.lo) if n is None else n
        return self.t[:, b.lo:b.lo + n]

    def bf16(self, b, n=None):
        ap = self.t[:, b.lo:b.hi].bitcast(BF16)
        return ap if n is None else ap[:, 0:n]

    def i32(self, b, n=None):
        ap = self.t[:, b.lo:b.hi].bitcast(I32)
        return ap if n is None else ap[:, 0:n]


def build_program(nt=NT_FULL, dbg=None):
    nc = bass.Bass("TRN2", target_bir_lowering=False)
    x_d = nc.dram_tensor("x", [S, D], F32, kind="ExternalInput").ap()
    mem_d = nc.dram_tensor("mem", [NMEM, D], F32, kind="ExternalInput").ap()
    pos_d = nc.dram_tensor("pos", [P, S], I32, kind="ExternalInput").ap()
    wflat_d = nc.dram_tensor("wflat", [NW], F32, kind="ExternalInput").ap()
    lngb_d = nc.dram_tensor("ln_gb", [3, P, 2 * D], F32, kind="ExternalInput").ap()
    lnT_d = nc.dram_tensor("ln_T", [P, 48], F32, kind="ExternalInput").ap()
    gatebT_d = nc.dram_tensor("gate_bT", [P, 24], F32, kind="ExternalInput").ap()
    gmln_d = nc.dram_tensor("gm_ln", [P, 1024], F32, kind="ExternalInput").ap()
    wsT_d = nc.dram_tensor("gm_wsT", [P, 512], F32, kind="ExternalInput").ap()
    bs_d = nc.dram_tensor("gm_bs", [P, 512], F32, kind="ExternalInput").ap()
    lam_d = nc.dram_tensor("lam", [P, 256], F32, kind="ExternalInput").ap()
    dng_d = nc.dram_tensor("dng", [P, 128], F32, kind="ExternalInput").ap()
    cst_d = nc.dram_tensor("consts", [P, 258], F32, kind="ExternalInput").ap()
    out_d = nc.dram_tensor("out", [S, D], F32, kind="ExternalOutput").ap()
    wbf_d = nc.dram_tensor("wbf", [NW], BF16, kind="Internal").ap()
    dbg_d = None
    if dbg:
        dbg_d = {k: nc.dram_tensor("dbg_" + k, list(shp), F32, kind="ExternalOutput").ap() for k, shp in dbg.items()}

    tr = Tracker()
    SCRW = 11264

    with ExitStack() as es:
        def sb(name, shape, dt):
            return es.enter_context(nc.sbuf_tensor(name, shape, dt))

        KT = sb("KT", [P, 4, S], BF16)
        VA = sb("VA", [P, S // P, 4, 130], BF16)
        xres = sb("xres", [P, 4, D], F32)
        xT = sb("xT", [P, KC, T], BF16)
        wring = sb("wring", [P, NSLOT, BLK], BF16)
        lngb = sb("lngb", [P, 2, D], F32)
        cst = sb("cst", [P, 258], F32)
        tri_bf = sb("tri_bf", [P, P], BF16)
        wsT_f = sb("wsT_f", [P, 512], F32)
        wsT_bf = sb("wsT_bf", [P, 4, P], BF16)
        bs_bc = sb("bs_bc", [P, 512], F32)
        gmln = sb("gmln", [P, 1024], F32)
        dngs = sb("dngs", [P, P], F32)
        gatebT = sb("gatebT", [P, 24], F32)
        lam_f = sb("lam_f", [P, 256], F32)
        KmT = sb("KmT", [P, 2, NMEM], BF16)
        Vm = sb("Vm", [P, 2, 4, 66], BF16)
        small = sb("small", [P, 256], F32)
        posi_t = sb("posi", [P, T], I32)
        qz = sb("qz", [P, 2, 4, T], BF16)
        qmz = sb("qmz", [P, 4, T], BF16)
        lnT = sb("lnT", [P, 48], F32)
        ki_t = sb("ki", [P, T], I32)
        scr_t = sb("scr", [P, SCRW], F32)
        banks_t = [es.enter_context(nc.psum_tensor(f"bank{i}", [P, 512], F32)) for i in range(8)]

        sem_c = {e: es.enter_context(nc.semaphore("sc_" + e)) for e in Tracker.ENGS}
        dnames = [f"w{s}" for s in range(NSLOT)] + [f"cv{i}" for i in range(NCHUNK)] + \
                 ["x0", "x1", "x2", "x3", "xs0", "xs1", "xs2", "xs3", "o0", "o1", "gb", "pos", "dbg"] + [f"c{i}" for i in range(10)]
        sem_d = {n: es.enter_context(nc.semaphore("sd_" + n)) for n in dnames}

        ident_f = cst[:, 0:128]
        tri_f = cst[:, 128:256]
        freq_ap = cst[:, 256:257]
        sign_ap = cst[:, 257:258]

        banks = [Buf(f"bank{i}", banks_t[i][:, :]) for i in range(8)]
        KTb = [Buf(f"KT{t}") for t in range(NT_FULL)]
        VAb = [Buf(f"VA{t}") for t in range(NT_FULL)]
        xresb = [Buf(f"xres{c}") for c in range(4)]
        xTb = [Buf(f"xT{c}") for c in range(4)]
        slotb = [Buf(f"slot{s}") for s in range(NSLOT)]
        cvb = [Buf(f"cv{i}") for i in range(NCHUNK)]
        lngbb = Buf("lngb")
        cstb = Buf("cst")
        miscb = {n: Buf(n) for n in ("tri_bf", "wsT_f", "wsT_bf", "bs_bc", "gmln", "dngs", "gatebT", "lam_f",
                                     "KmT", "Vm", "neglam", "mhalf", "posi", "ki", "qz", "qmz", "lnT")}
        sc = Scratch(scr_t, SCRW)

        small_state = {"i": 0}
        NSM = 10
        smallb = [Buf(f"small{i}") for i in range(NSM)]

        def small_alloc():
            i = small_state["i"] % NSM
            small_state["i"] += 1
            return smallb[i], small[:, 8 + i * 24: 32 + i * 24]

        neglam_ap = small[:, 0:1]
        mhalf_ap = small[:, 1:2]
        lamtmp = small[:, 2:8]

        def nfree(ap):
            n_ = 1
            for d_ in ap.shape[1:]:
                n_ *= d_
            return n_

        DTB = {F32: 4, BF16: 2, I32: 4}

        def MM(out, lhsT, rhs, start, stop, reads, writes, skip=False):
            tr.add("pe", lambda e: e.matmul(out, lhsT=lhsT, rhs=rhs, start=start, stop=stop,
                                            skip_group_check=skip), reads, writes,
                   cost=0.012 + max(nfree(rhs), 64) / 2400.0)

        def TRP(out, in_, reads, writes):
            tr.add("pe", lambda e: e.transpose(out, in_, ident_f), list(reads) + [cstb], writes, cost=0.11)

        def ACT(out, in_, func, reads, writes, bias=None, scale=None):
            kw = {}
            if bias is not None:
                kw["bias"] = bias
            if scale is not None:
                kw["scale"] = scale
            grp = None if func in (AF.Identity, AF.Copy) else func
            tr.add("act", lambda e: e.activation(out=out, in_=in_, func=func, **kw), reads, writes,
                   cost=0.2 + nfree(out) / 1400.0, grp=grp)

        def ecost(eng, out):
            n_ = nfree(out)
            if eng == "pool":
                return 0.35 if n_ <= 8 else 0.3 + n_ / 620.0
            return 0.1 + n_ / 960.0

        def TT(eng, out, in0, in1, op, reads, writes):
            tr.add(eng, lambda e: e.tensor_tensor(out=out, in0=in0, in1=in1, op=op), reads, writes,
                   cost=ecost(eng, out))

        def TS(eng, out, in0, s1, s2, op0, op1, reads, writes):
            if op1 is None:
                tr.add(eng, lambda e: e.tensor_scalar(out=out, in0=in0, scalar1=s1, scalar2=None, op0=op0),
                       reads, writes, cost=ecost(eng, out))
            else:
                tr.add(eng, lambda e: e.tensor_scalar(out=out, in0=in0, scalar1=s1, scalar2=s2, op0=op0, op1=op1),
                       reads, writes, cost=ecost(eng, out))

        def STT(out, in0, scalar, in1, op0, op1, reads, writes):
            tr.add("dve", lambda e: e.scalar_tensor_tensor(out=out, in0=in0, scalar=scalar, in1=in1,
                                                           op0=op0, op1=op1), reads, writes, cost=ecost("dve", out))

        def CP(eng, out, in_, reads, writes):
            if eng == "act":
                ACT(out, in_, AF.Identity, reads, writes)
            else:
                tr.add(eng, lambda e: e.tensor_copy(out=out, in_=in_), reads, writes, cost=ecost(eng, out))

        def DMA(out, in_, sem, reads, writes, eng="sp"):
            nb = nfree(out) * out.shape[0] * DTB.get(out.dtype, 4) + nfree(in_) * (in_.shape[0] if in_.shape[0] > 1 else 1) * 0
            if eng == "pool":
                nb = nb * 3
            return tr.add(eng, lambda e: e.dma_start(out=out, in_=in_), reads, writes, dma=sem,
                          cost=0.1 if eng == "sp" else 1.0, nbytes=nb)

        evac_state = {"i": 0}

        def EVAC(out, in_, reads, writes):
            eng = "act" if evac_state["i"] % 2 == 0 else "dve"
            evac_state["i"] += 1
            CP(eng, out, in_, reads, writes)

        ws_state = {"i": 0}

        cv_state = {"n": 0}
        CV_AHEAD = 2

        def ensure_converted(upto, after=None):
            upto = min(upto, NCHUNK - 1)
            while cv_state["n"] <= upto:
                i = cv_state["n"]
                csrc = wflat_d[i * CH:(i + 1) * CH].rearrange("(a b) -> a b", b=2048)
                cdst = wbf_d[i * CH:(i + 1) * CH].rearrange("(a b) -> a b", b=2048)
                ex = [after] if (after is not None and i >= 1) else []
                tr.add("pool", lambda e, cdst=cdst, csrc=csrc: e.dma_start(out=cdst, in_=csrc), [], [cvb[i]],
                       dma=f"cv{i}", extra=ex, cost=1.0, nbytes=CH * 6)
                cv_state["n"] += 1

        last_wload = {"ev": None}

        def wload(name):
            s = ws_state["i"] % NSLOT
            ws_state["i"] += 1
            off, size = WB[name]
            lo, hi = off * P, (off + size) * P
            c0, c1 = lo // CH, (hi - 1) // CH
            ensure_converted(c1, last_wload["ev"])
            cvs = [cvb[i] for i in range(c0, c1 + 1)]
            src = wbf_d[lo:hi].rearrange("(p f) -> p f", f=size)
            last_wload["ev"] = DMA(wring[:, s, 0:size], src, f"w{s}", cvs, [slotb[s]])
            ensure_converted(c1 + CV_AHEAD, last_wload["ev"])
            return s

        def wview(s, size, pattern, **kw):
            return wring[:, s, 0:size].rearrange(pattern, **kw)

        dbg_cnt = {"i": 0}

        def DUMP(key, ap, reads):
            if dbg_d is not None and key in dbg_d:
                DMA(dbg_d[key], ap, "dbg", reads, [])

        ensure_converted(0)

        DMA(cst[:, :], cst_d[:, :], "c0", [], [cstb])
        DMA(wsT_f[:, :], wsT_d[:, :], "c1", [], [miscb["wsT_f"]])
        DMA(bs_bc[:, :], bs_d[:, :], "c2", [], [miscb["bs_bc"]])
        DMA(gmln[:, :], gmln_d[:, :], "c3", [], [miscb["gmln"]])
        DMA(dngs[:, :], dng_d[:, :], "c4", [], [miscb["dngs"]])
        DMA(gatebT[:, :], gatebT_d[:, :], "c5", [], [miscb["gatebT"]])
        DMA(lam_f[:, :], lam_d[:, :], "c6", [], [miscb["lam_f"]])
        DMA(lnT[:, :], lnT_d[:, :], "c7", [], [miscb["lnT"]])

        CP("dve", tri_bf[:, :], tri_f, [cstb], [miscb["tri_bf"]])
        for g in range(4):
            TT("dve", wsT_bf[:, g, :], wsT_f[:, g * 128:(g + 1) * 128], tri_f, ALU.mult,
               [miscb["wsT_f"], cstb], [miscb["wsT_bf"]])
        TS("dve", dngs[:, :], dngs[:, :], float(1.0 - LAM_INIT), None, ALU.mult, None,
           [miscb["dngs"]], [miscb["dngs"]])
        tr.add("pool", lambda e: e.memset(mhalf_ap, -0.5), [], [miscb["mhalf"]])
        tr.add("pool", lambda e: e.memset(VA[:, :, :, 128:130], 1.0), [], VAb)
        tr.add("pool", lambda e: e.memset(Vm[:, :, :, 64:66], 1.0), [], [miscb["Vm"]])
        tr.add("pool", lambda e: e.memset(qz[:, :, :, :], 0.0), [], [miscb["qz"]], cost=5.0)
        tr.add("pool", lambda e: e.memset(qmz[:, :, :], 0.0), [], [miscb["qmz"]], cost=2.5)
        lb = miscb["lam_f"]
        nb_ = miscb["neglam"]
        ptmp = sc.alloc(128, "lamp")
        pt = sc.f32(ptmp)
        TT("dve", pt[:, 0:64], lam_f[:, 0:64], lam_f[:, 64:128], ALU.mult, [lb], [ptmp])
        TT("dve", pt[:, 64:128], lam_f[:, 128:192], lam_f[:, 192:256], ALU.mult, [lb, ptmp], [ptmp])
        tr.add("dve", lambda e: e.tensor_reduce(out=lamtmp[:, 0:1], in_=pt[:, 0:64], axis=AX.X, op=ALU.add),
               [ptmp], [nb_])
        tr.add("dve", lambda e: e.tensor_reduce(out=lamtmp[:, 1:2], in_=pt[:, 64:128], axis=AX.X, op=ALU.add),
               [ptmp, nb_], [nb_])
        ACT(lamtmp[:, 2:4], lamtmp[:, 0:2], AF.Exp, [nb_], [nb_])
        TT("dve", lamtmp[:, 4:5], lamtmp[:, 2:3], lamtmp[:, 3:4], ALU.subtract, [nb_], [nb_])
        TS("dve", neglam_ap, lamtmp[:, 4:5], -1.0, float(-LAM_INIT), ALU.mult, ALU.add, [nb_], [nb_])
        sc.free(ptmp)

        memtm = sc.alloc(2 * D, "memtm")
        memTb = sc.alloc(KC * NMEM // 2, "memT")
        mt = sc.f32(memtm).rearrange("p (a b) -> p a b", b=D)
        mT = sc.bf16(memTb).rearrange("p (a b) -> p a b", b=NMEM)
        for mc in range(2):
            DMA(mt[:, mc, :], mem_d[mc * P:(mc + 1) * P, :], f"c{8 + mc}", [], [memtm])
        for kc in range(KC):
            bk = banks[kc % 2]
            for mc in range(2):
                TRP(bk.ap[:, mc * P:(mc + 1) * P], mt[:, mc, kc * P:(kc + 1) * P], [memtm], [bk])
            EVAC(mT[:, kc, :], bk.ap[:, 0:NMEM], [bk], [memTb])
        s = wload("mk")
        wv = wview(s, 2048, "p (c k n) -> p c k n", c=2, k=KC)
        for cb in range(2):
            bk = banks[2 + cb]
            for kc in range(KC):
                MM(bk.ap[:, 0:NMEM], wv[:, cb, kc, :], mT[:, kc, :], kc == 0, kc == KC - 1,
                   [slotb[s], memTb], [bk])
            EVAC(KmT[:, cb, :], bk.ap[:, 0:NMEM], [bk], [miscb["KmT"]])
        s = wload("mv")
        wv = wview(s, 2048, "p (k n) -> p k n", k=KC)
        for mc in range(2):
            bk = banks[4 + mc]
            for kc in range(KC):
                MM(bk.ap[:, 0:256], mT[:, kc, mc * P:(mc + 1) * P], wv[:, kc, :], kc == 0, kc == KC - 1,
                   [slotb[s], memTb], [bk])
            EVAC(Vm[:, mc, :, 0:64], bk.ap[:, 0:256].rearrange("p (h e) -> p h e", e=64), [bk], [miscb["Vm"]])
        sc.free(memtm)
        sc.free(memTb)

        def stage_x_load(tile):
            bufs = []
            for c in range(4):
                b = sc.alloc(D, f"xst{c}")
                r0 = tile * T + c * P
                DMA(sc.f32(b), x_d[r0:r0 + P, :], f"xs{c}", [], [b])
                bufs.append(b)
            return bufs

        def stage_x_transpose(bufs):
            for c in range(4):
                src_ap = sc.f32(bufs[c])
                for g in range(2):
                    bk = banks[(2 * c + g) % 8]
                    for k4 in range(4):
                        kc = g * 4 + k4
                        TRP(bk.ap[:, k4 * P:(k4 + 1) * P], src_ap[:, kc * P:(kc + 1) * P], [bufs[c]], [bk])
                    EVAC(xT[:, g * 4:(g + 1) * 4, c * P:(c + 1) * P],
                         bk.ap.rearrange("p (a b) -> p a b", b=P), [bk], [xTb[c]])
                sc.free(bufs[c])

        def load_xres(tile):
            for c in range(4):
                r0 = tile * T + c * P
                DMA(xres[:, c, :], x_d[r0:r0 + P, :], f"x{c}", [], [xresb[c]])

        def transpose_chunk(c, bks, l):
            for g in range(2):
                bk = bks[g]
                for k4 in range(4):
                    kc = g * 4 + k4
                    TRP(bk.ap[:, k4 * P:(k4 + 1) * P], xres[:, c, kc * P:(kc + 1) * P], [xresb[c]], [bk])
                for k4 in range(4):
                    kc = g * 4 + k4
                    o_ = xT[:, kc, c * P:(c + 1) * P]
                    i_ = bk.ap[:, k4 * P:(k4 + 1) * P]
                    ga = lnT[:, l * 16 + kc:l * 16 + kc + 1]
                    be = lnT[:, l * 16 + 8 + kc:l * 16 + 9 + kc]
                    if k4 % 2 == 0:
                        ACT(o_, i_, AF.Identity, [bk, miscb["lnT"]], [xTb[c]], bias=be, scale=ga)
                    else:
                        TS("dve", o_, i_, ga, be, ALU.mult, ALU.add, [bk, miscb["lnT"]], [xTb[c]])

        def load_gb(l):
            DMA(lngb[:, :, :], lngb_d[l].rearrange("p (a b) -> p a b", b=D), "gb", [], [lngbb])

        def ln_norm(c, fb):
            xr = xres[:, c, :]
            xb = xresb[c]
            for hf in range(2):
                STT(xr[:, hf * 512:(hf + 1) * 512], xr[:, hf * 512:(hf + 1) * 512], ALPHA, fb[hf].ap,
                    ALU.mult, ALU.add, [xb, fb[hf]], [xb])
            smb3, sm3 = small_alloc()
            st = sm3[:, 8:20]
            for hf in range(2):
                tr.add("dve", lambda e, hf=hf: e.bn_stats(out=st[:, hf * 6:(hf + 1) * 6],
                                                           in_=xr[:, hf * 512:(hf + 1) * 512]),
                       [xb], [smb3], cost=0.7)
            mv = sm3[:, 0:2]
            tr.add("dve", lambda e: e.bn_aggr(out=mv, in_=st), [smb3], [smb3], cost=0.2)
            ve = sm3[:, 2:3]
            rstd = sm3[:, 3:4]
            nmr = sm3[:, 4:5]
            TS("pool", ve, mv[:, 1:2], float(EPS), None, ALU.add, None, [smb3], [smb3])
            TT("pool", rstd, ve, mhalf_ap, ALU.pow, [smb3, miscb["mhalf"]], [smb3])
            TS("pool", nmr, mv[:, 0:1], rstd, -1.0, ALU.mult, ALU.mult, [smb3], [smb3])
            ACT(xr, xr, AF.Identity, [xb, smb3], [xb], bias=nmr, scale=rstd)

        def ln_affine(c, dst_ap, dst_bufs):
            xr = xres[:, c, :]
            xb = xresb[c]
            TT("pool", xr, xr, lngb[:, 0, :], ALU.mult, [xb, lngbb], [xb])
            TT("pool", dst_ap, xr, lngb[:, 1, :], ALU.add, [xb, lngbb], dst_bufs)

        GU_BANKS = [(4, 5), (6, 7), (0, 1), (2, 3)]

        def ffn_s1(pre, inter=None):
            tr.tag = pre + ".s1"
            hTb = sc.alloc(NJ * T // 2, "hT")
            hT = sc.bf16(hTb).rearrange("p (j t) -> p j t", t=T)
            sgb = [sc.alloc(T, "sg0"), sc.alloc(T, "sg1")]
            for j in range(NJ):
                s = wload(f"{pre}_in{j}")
                wv = wview(s, 2048, "p (c k n) -> p c k n", c=2, k=KC)
                bg, bu = banks[GU_BANKS[j % 4][0]], banks[GU_BANKS[j % 4][1]]
                if j < NSPLIT:
                    for hf_ in range(2):
                        cs = slice(hf_ * 256, (hf_ + 1) * 256)
                        rd = [slotb[s], xTb[2 * hf_], xTb[2 * hf_ + 1]]
                        for kc in range(KC):
                            MM(bg.ap[:, cs], wv[:, 0, kc, :], xT[:, kc, cs], kc == 0, kc == KC - 1, rd, [bg])
                        for kc in range(KC):
                            MM(bu.ap[:, cs], wv[:, 1, kc, :], xT[:, kc, cs], kc == 0, kc == KC - 1, rd, [bu])
                else:
                    for kc in range(KC):
                        MM(bg.ap, wv[:, 0, kc, :], xT[:, kc, :], kc == 0, kc == KC - 1, [slotb[s]] + xTb, [bg])
                    for kc in range(KC):
                        MM(bu.ap, wv[:, 1, kc, :], xT[:, kc, :], kc == 0, kc == KC - 1, [slotb[s]] + xTb, [bu])
                sg = sgb[j % 2]
                ACT(sc.f32(sg), bg.ap, AF.Silu, [bg], [sg])
                STT(hT[:, j, :], bu.ap, 0.5, sc.f32(sg), ALU.mult, ALU.mult, [bu, sg], [hTb])
                if inter and j in inter:
                    tg = tr.tag
                    inter[j]()
                    tr.tag = tg
            sc.free(sgb[0])
            sc.free(sgb[1])
            return hTb

        PBS = [banks[4:8], banks[0:4]]

        def fbanks(c):
            pb = PBS[c // 2]
            return [pb[(c % 2) * 2], pb[(c % 2) * 2 + 1]]

        def ffn_s3(pre, hTb):
            tr.tag = pre + ".s3"
            hT = sc.bf16(hTb).rearrange("p (j t) -> p j t", t=T)
            for pair in range(2):
                pb = PBS[pair]
                for jj in range(NJ // 2):
                    s = wload(f"{pre}_out{jj}")
                    wv = wview(s, 2048, "p (j n) -> p j n", j=2)
                    for jl in range(2):
                        j = 2 * jj + jl
                        for cl in range(2):
                            c = 2 * pair + cl
                            for hf in range(2):
                                MM(pb[cl * 2 + hf].ap, hT[:, j, c * P:(c + 1) * P], wv[:, jl, hf * 512:(hf + 1) * 512],
                                   j == 0, j == NJ - 1, [slotb[s], hTb], [pb[cl * 2 + hf]])
            sc.free(hTb)

        def ln_stage(tagname, l, after_T=None):
            for c in range(4):
                tr.tag = tagname
                ln_norm(c, fbanks(c))
                transpose_chunk(c, fbanks(c), l)
                ln_affine(c, xres[:, c, :], [xresb[c]])
                if after_T:
                    after_T(c)

        out_events = []

        def ln3_chunk(tile, c):
            tg = tr.tag
            tr.tag = "ln3"
            ob = sc.alloc(D, "ostage")
            ln_norm(c, fbanks(c))
            ln_affine(c, sc.f32(ob), [ob])
            r0 = tile * T + c * P
            ev = DMA(out_d[r0:r0 + P, :], sc.f32(ob), f"o{c % 2}", [ob], [])
            out_events.append(ev)
            sc.free(ob)
            tr.tag = tg

        def rope_tables(tile):
            tr.tag = "rope"
            pib = miscb["posi"]
            kib = miscb["ki"]
            DMA(posi_t[:, :], pos_d[:, tile * T:(tile + 1) * T], "pos", [], [pib])
            angb = sc.alloc(T, "ang")
            ang = sc.f32(angb)
            CP("dve", ang, posi_t[:, :], [pib], [angb])
            TS("dve", ang, ang, freq_ap, None, ALU.mult, None, [angb, cstb], [angb])
            outs = []
            for which in range(2):
                a2b = sc.alloc(T, "ang2")
                a2 = sc.f32(a2b)
                if which == 1:
                    TS("dve", a2, ang, float(math.pi / 2), None, ALU.add, None, [angb], [a2b])
                    srcb, srca = a2b, a2
                else:
                    srcb, srca = angb, ang
                kfb = sc.alloc(T, "kf")
                rb = sc.alloc(T, "rr")
                TS("dve", ki_t[:, :], srca, float(1.0 / TWO_PI), None, ALU.mult, None, [srcb], [kib])
                CP("dve", sc.f32(kfb), ki_t[:, :], [kib], [kfb])
                r = sc.f32(rb)
                STT(r, sc.f32(kfb), float(-CW1), srca, ALU.mult, ALU.add, [kfb, srcb], [rb])
                STT(r, sc.f32(kfb), float(-CW2), r, ALU.mult, ALU.add, [kfb, rb], [rb])
                TS("dve", r, r, float(-PI_LO), float(PI_LO), ALU.max, ALU.min, [rb], [rb])
                ob = sc.alloc(T, "ropetab")
                if which == 0:
                    ACT(sc.f32(ob), r, AF.Sin, [rb, cstb], [ob], scale=sign_ap)
                else:
                    ACT(sc.f32(ob), r, AF.Sin, [rb], [ob])
                sc.free(rb)
                sc.free(kfb)
                sc.free(a2b)
                outs.append(ob)
            sc.free(angb)
            return outs[1], outs[0]

        def mixer(tile, Cb, Sb):
            Ct, St = sc.f32(Cb), sc.f32(Sb)
            vnb = sc.alloc(4 * T // 2, "vn")
            vn = sc.bf16(vnb).rearrange("p (c t) -> p c t", t=512)
            sv0, sv1 = wload("pv0"), wload("pv1")
            sd0, sd1 = wload("pd0"), wload("pd1")
            wvv = [wview(sv0, 2048, "p (k n) -> p k n", k=4), wview(sv1, 2048, "p (k n) -> p k n", k=4)]
            wvd = [wview(sd0, 2048, "p (k n) -> p k n", k=4), wview(sd1, 2048, "p (k n) -> p k n", k=4)]

            def v_and_vd(c):
                fb = fbanks(c)
                tr.tag = "v"
                bk = fb[0]
                for kc in range(KC):
                    MM(bk.ap, xT[:, kc, c * P:(c + 1) * P], wvv[kc // 4][:, kc % 4, :], kc == 0, kc == KC - 1,
                       [slotb[sv0], slotb[sv1], xTb[c]], [bk])
                gvb = sc.alloc(T, "gv")
                gv = sc.f32(gvb)
                ACT(gv, bk.ap, AF.Gelu_apprx_tanh, [bk], [gvb])
                smb, sm = small_alloc()
                tr.add("dve", lambda e, sm=sm, gv=gv: e.bn_stats(out=sm[:, 8:14], in_=gv), [gvb], [smb], cost=0.7)
                mv = sm[:, 0:2]
                tr.add("dve", lambda e, mv=mv, sm=sm: e.bn_aggr(out=mv, in_=sm[:, 8:14]), [smb], [smb], cost=0.2)
                ve, rstd, nmr = sm[:, 2:3], sm[:, 3:4], sm[:, 4:5]
                TS("pool", ve, mv[:, 1:2], float(EPS), None, ALU.add, None, [smb], [smb])
                TT("pool", rstd, ve, mhalf_ap, ALU.pow, [smb, miscb["mhalf"]], [smb])
                TS("pool", nmr, mv[:, 0:1], rstd, -1.0, ALU.mult, ALU.mult, [smb], [smb])
                ACT(gv, gv, AF.Identity, [gvb, smb], [gvb], bias=nmr, scale=rstd)
                TT("pool", gv, gv, gmln[:, 0:512], ALU.mult, [gvb, miscb["gmln"]], [gvb])
                TT("pool", vn[:, c, :], gv, gmln[:, 512:1024], ALU.add, [gvb, miscb["gmln"]], [vnb])
                sc.free(gvb)
                tr.tag = "vd"
                bk = fb[1]
                for kc in range(KC):
                    MM(bk.ap, xT[:, kc, c * P:(c + 1) * P], wvd[kc // 4][:, kc % 4, :], kc == 0, kc == KC - 1,
                       [slotb[sd0], slotb[sd1], xTb[c]], [bk])
                EVAC(VA[:, tile * 4 + c, :, 0:128], bk.ap.rearrange("p (h e) -> p h e", e=128), [bk], [VAb[tile]])

            ln_stage("f1.ln", 0, v_and_vd)

            tr.tag = "u"
            uTb = sc.alloc(4 * T, "uT")
            uT = sc.f32(uTb).rearrange("p (c t) -> p c t", t=T)
            bi = 0
            for b in range(2):
                s = wload(f"pu{b}")
                wv = wview(s, 2048, "p (c k n) -> p c k n", c=2, k=KC)
                for cbl in range(2):
                    cb = 2 * b + cbl
                    bk = banks[bi % 8]
                    bi += 1
                    for hf_ in range(2):
                        cs = slice(hf_ * 256, (hf_ + 1) * 256)
                        rd = [slotb[s], xTb[2 * hf_], xTb[2 * hf_ + 1]]
                        for kc in range(KC):
                            MM(bk.ap[:, cs], wv[:, cbl, kc, :], xT[:, kc, cs], kc == 0, kc == KC - 1, rd, [bk])
                    ACT(uT[:, cb, :], bk.ap, AF.Gelu_apprx_tanh, [bk], [uTb])
            tr.tag = "qk"
            for kind in ("q", "k"):
                for h in range(4):
                    s = wload(f"p{kind}{h}")
                    wv = wview(s, 2048, "p (c k n) -> p c k n", c=2, k=KC)
                    bq, bp = banks[bi % 8], banks[(bi + 1) % 8]
                    bi += 2
                    for kc in range(KC):
                        MM(bq.ap, wv[:, 0, kc, :], xT[:, kc, :], kc == 0, kc == KC - 1, [slotb[s]] + xTb, [bq])
                    for kc in range(KC):
                        MM(bp.ap, wv[:, 1, kc, :], xT[:, kc, :], kc == 0, kc == KC - 1, [slotb[s]] + xTb, [bp])
                    t1b, t2b = sc.alloc(T, "rt1"), sc.alloc(T, "rt2")
                    TT("dve", sc.f32(t1b), bq.ap, Ct, ALU.mult, [bq, Cb], [t1b])
                    TT("dve", sc.f32(t2b), bp.ap, St, ALU.mult, [bp, Sb], [t2b])
                    if kind == "q":
                        TT("pool", qz[0:64, 0, h, :], sc.f32(t1b)[0:64, :], sc.f32(t2b)[0:64, :], ALU.add,
                           [t1b, t2b], [miscb["qz"]])
                        TT("pool", qz[64:128, 1, h, :], sc.f32(t1b)[64:128, :], sc.f32(t2b)[64:128, :], ALU.add,
                           [t1b, t2b], [miscb["qz"]])
                    else:
                        TT("pool", KT[:, h, tile * T:(tile + 1) * T], sc.f32(t1b), sc.f32(t2b), ALU.add,
                           [t1b, t2b], [KTb[tile]])
                    sc.free(t1b)
                    sc.free(t2b)
            sc.free(Cb)
            sc.free(Sb)
            tr.tag = "qm"
            s = wload("pm")
            wv = wview(s, 2048, "p (c k n) -> p c k n", c=2, k=KC)
            for cbl in range(2):
                bk = banks[bi % 8]
                bi += 1
                for kc in range(KC):
                    MM(bk.ap, wv[:, cbl, kc, :], xT[:, kc, :], kc == 0, kc == KC - 1, [slotb[s]] + xTb, [bk])
                CP("act", qmz[0:64, 2 * cbl, :], bk.ap[0:64, :], [bk], [miscb["qmz"]])
                CP("dve", qmz[64:128, 2 * cbl + 1, :], bk.ap[64:128, :], [bk], [miscb["qmz"]])
            tr.tag = "gmlp"
            ygmb = sc.alloc(4 * T // 2, "ygmT")
            ygmT = sc.bf16(ygmb).rearrange("p (c t) -> p c t", t=T)
            for c in range(4):
                bk = banks[bi % 8]
                bi += 1
                for g in range(4):
                    MM(bk.ap[:, g * P:(g + 1) * P], vn[:, c, g * P:(g + 1) * P], wsT_bf[:, g, :], True, True,
                       [vnb, miscb["wsT_bf"]], [bk])
                tmb = sc.alloc(T, "gmt")
                tm = sc.f32(tmb)
                TT("dve", tm, bk.ap, bs_bc[:, :], ALU.add, [bk, miscb["bs_bc"]], [tmb])
                TT("pool", ygmT[:, :, c * P:(c + 1) * P], tm.rearrange("p (g t) -> p g t", t=P),
                   uT[:, :, c * P:(c + 1) * P], ALU.mult, [tmb, uTb], [ygmb])
                sc.free(tmb)
            sc.free(vnb)
            sc.free(uTb)
            tr.tag = "attn"
            ydTb = sc.alloc(4 * T // 2, "ydT")
            ydT = sc.bf16(ydTb).rearrange("p (c t) -> p c t", t=T)
            Eb = [[sc.alloc(T // 2, f"E{r}{q}") for q in range(2)] for r in range(2)]
            nk = 4 * tile + 4
            ASETS = [banks[0:3], banks[5:8]]
            SBK = [banks[3], banks[4]]

            def acc(A, r, qs):
                if qs < 3:
                    return A[r], A[r].ap[:, qs * 130: qs * 130 + 129]
                return A[2], A[2].ap[:, r * 130: r * 130 + 129]

            pending_T = []

            def flush_T():
                for (hh, ytb_, TBk) in pending_T:
                    ytm_ = sc.f32(ytb_).rearrange("p (q e) -> p q e", e=P)
                    for qs in range(4):
                        TRP(TBk.ap[:, qs * P:(qs + 1) * P], ytm_[:, qs, :], [ytb_], [TBk])
                    CP("dve", ydT[:, hh, :], TBk.ap, [TBk], [ydTb])
                    sc.free(ytb_)
                pending_T.clear()

            for h in range(4):
                tr.tag = "attn"
                A = ASETS[h % 2]
                started = set()

                def qk(kc, h=h):
                    i = kc - 4 * tile
                    q0 = P * i if i > 0 else 0
                    for r in range(2):
                        bk = SBK[r]
                        MM(bk.ap[:, q0:T], KT[:, h, kc * P:(kc + 1) * P],
                           qz[:, r, h, q0:T], True, True, [KTb[kc // 4], miscb["qz"]], [bk])
                        eb = Eb[r][kc % 2]
                        E = sc.bf16(eb)
                        ACT(E[:, q0:T], bk.ap[:, q0:T], AF.Exp, [bk], [eb], scale=0.125)
                        if i >= 0:
                            TT("pool", E[:, q0:q0 + P], E[:, q0:q0 + P], tri_bf[:, :], ALU.mult,
                               [eb, miscb["tri_bf"]], [eb])

                def av(kc, h=h, started=started, A=A):
                    i = kc - 4 * tile
                    for r in range(2):
                        eb = Eb[r][kc % 2]
                        E = sc.bf16(eb)
                        for qs in range(max(i, 0), 4):
                            ab, aap = acc(A, r, qs)
                            first = ab.name not in started
                            started.add(ab.name)
                            MM(aap, E[:, qs * P:(qs + 1) * P], VA[:, kc, h, 0:129], first, kc == 4 * tile + qs,
                               [eb, VAb[kc // 4]], [ab], skip=True)

                qk(0)
                for kc in range(nk):
                    if kc + 1 < nk:
                        qk(kc + 1)
                    av(kc)
                    if kc == min(2, nk - 1):
                        flush_T()
                        tr.tag = "attn"
                tr.tag = "attn.evac"
                ytb = sc.alloc(4 * P, "ytm")
                ytm = sc.f32(ytb).rearrange("p (q e) -> p q e", e=P)
                ddbs, sms = [], []
                for qs in range(4):
                    b1, a1 = acc(A, 0, qs)
                    b2, a2 = acc(A, 1, qs)
                    smb, sm = small_alloc()
                    tr.add("dve", lambda e, sm=sm, a1=a1: e.reciprocal(out=sm[:, 0:1], in_=a1[:, 128:129]), [b1], [smb], cost=0.15)
                    tr.add("dve", lambda e, sm=sm, a2=a2: e.reciprocal(out=sm[:, 1:2], in_=a2[:, 128:129]), [b2, smb], [smb], cost=0.15)
                    TT("dve", sm[:, 2:3], sm[:, 1:2], neglam_ap, ALU.mult, [smb, miscb["neglam"]], [smb])
                    dtb, ddb = sc.alloc(P, "dt"), sc.alloc(P, "dd")
                    TS("dve", sc.f32(dtb), a1[:, 0:128], sm[:, 0:1], None, ALU.mult, None, [b1, smb], [dtb])
                    STT(sc.f32(ddb), a2[:, 0:128], sm[:, 2:3], sc.f32(dtb), ALU.mult, ALU.add, [b2, smb, dtb], [ddb])
                    dd = sc.f32(ddb)
                    dtf = sc.f32(dtb)
                    TT("pool", dtf, dd, dd, ALU.mult, [ddb, dtb], [dtb])
                    tr.add("dve", lambda e, dtf=dtf, sm=sm: e.tensor_reduce(out=sm[:, 3:4], in_=dtf, axis=AX.X, op=ALU.add),
                           [dtb, smb], [smb], cost=0.25)
                    sc.free(dtb)
                    ddbs.append(ddb)
                    sms.append((smb, sm))
                for qs in range(4):
                    smb, sm = sms[qs]
                    TS("pool", sm[:, 4:5], sm[:, 3:4], float(1.0 / 128), float(EPS), ALU.mult, ALU.add, [smb], [smb])
                    TT("pool", sm[:, 5:6], sm[:, 4:5], mhalf_ap, ALU.pow, [smb, miscb["mhalf"]], [smb])
                for qs in range(4):
                    smb, sm = sms[qs]
                    STT(ytm[:, qs, :], sc.f32(ddbs[qs]), sm[:, 5:6], dngs[:, :], ALU.mult, ALU.mult,
                        [ddbs[qs], smb, miscb["dngs"]], [ytb])
                    sc.free(ddbs[qs])
                pending_T.append((h, ytb, A[2]))
            tr.tag = "mem"
            ymb = sc.alloc(4 * 256, "ymtm")
            ymtm = sc.f32(ymb).rearrange("p (q e) -> p q e", e=256)
            MA = [banks[0], banks[1]]
            units = [(h, mc) for h in range(4) for mc in range(2)]

            def mqk(u):
                h, mc = units[u]
                bk = SBK[u % 2]
                MM(bk.ap, KmT[:, h // 2, mc * P:(mc + 1) * P], qmz[:, h, :], True, True,
                   [miscb["KmT"], miscb["qmz"]], [bk])
                eb = Eb[0][u % 2]
                ACT(sc.bf16(eb), bk.ap, AF.Exp, [bk], [eb], scale=0.125)

            def mav(u):
                h, mc = units[u]
                ab = MA[h % 2]
                eb = Eb[0][u % 2]
                E = sc.bf16(eb)
                for qs in range(4):
                    MM(ab.ap[:, qs * 66:qs * 66 + 65], E[:, qs * P:(qs + 1) * P], Vm[:, mc, h, 0:65],
                       mc == 0 and qs == 0, mc == 1, [eb, miscb["Vm"]], [ab], skip=True)
                if mc == 1:
                    for qs in range(4):
                        smb, sm = small_alloc()
                        aap = ab.ap[:, qs * 66:qs * 66 + 65]
                        tr.add("dve", lambda e, sm=sm, aap=aap: e.reciprocal(out=sm[:, 0:1], in_=aap[:, 64:65]), [ab], [smb], cost=0.15)
                        TS("dve", ymtm[:, qs, h * 64:(h + 1) * 64], aap[:, 0:64], sm[:, 0:1], None, ALU.mult, None,
                           [ab, smb], [ymb])

            mqk(0)
            for u in range(len(units)):
                if u + 1 < len(units):
                    mqk(u + 1)
                mav(u)
                if u == 2:
                    flush_T()
                    tr.tag = "mem"
            for r in range(2):
                for q in range(2):
                    sc.free(Eb[r][q])
            ymTb = sc.alloc(2 * T // 2, "ymT")
            ymT = sc.bf16(ymTb).rearrange("p (c t) -> p c t", t=T)
            for cc in range(2):
                bk = banks[6 + cc]
                for qs in range(4):
                    TRP(bk.ap[:, qs * P:(qs + 1) * P], ymtm[:, qs, cc * P:(cc + 1) * P], [ymb], [bk])
                EVAC(ymT[:, cc, :], bk.ap, [bk], [ymTb])
            sc.free(ymb)
            tr.tag = "merge"
            mgb = sc.alloc(KC * T // 2, "mergedT")
            mgT = sc.bf16(mgb).rearrange("p (c t) -> p c t", t=T)
            gtb = [sc.alloc(T, f"gt{b}") for b in range(3)]
            mtb = [sc.alloc(T, f"mt{b}") for b in range(2)]
            for nb in range(8):
                sa, sb_, scw = wload(f"mg{nb}a"), wload(f"mg{nb}b"), wload(f"mg{nb}c")
                wa = wview(sa, 2048, "p (c k n) -> p c k n", c=2, k=KC)
                wb_ = wview(sb_, 1024, "p (k n) -> p k n", k=KC)
                wc = wview(scw, 1280, "p (i n) -> p i n", i=10)
                for b in range(3):
                    bk = banks[b]
                    for kc in range(KC):
                        lhs = wa[:, b, kc, :] if b < 2 else wb_[:, kc, :]
                        MM(bk.ap, lhs, xT[:, kc, :], kc == 0, kc == KC - 1, [slotb[sa], slotb[sb_]] + xTb, [bk])
                    ACT(sc.f32(gtb[b]), bk.ap, AF.Sigmoid, [bk, miscb["gatebT"]], [gtb[b]],
                        bias=gatebT[:, b * 8 + nb:b * 8 + nb + 1])
                srcs = [(ygmT, ygmb, 4, 0), (ydT, ydTb, 4, 4), (ymT, ymTb, 2, 8)]
                for b, (src_, srcb, ndc, i0) in enumerate(srcs):
                    bk = banks[3 + b]
                    for dc in range(ndc):
                        MM(bk.ap, wc[:, i0 + dc, :], src_[:, dc, :], dc == 0, dc == ndc - 1, [slotb[scw], srcb], [bk])
                m0, m1 = sc.f32(mtb[0]), sc.f32(mtb[1])
                TT("dve", m0, sc.f32(gtb[0]), banks[3].ap, ALU.mult, [gtb[0], banks[3]], [mtb[0]])
                TT("dve", m1, sc.f32(gtb[1]), banks[4].ap, ALU.mult, [gtb[1], banks[4]], [mtb[1]])
                TT("pool", m0, m0, m1, ALU.add, [mtb[0], mtb[1]], [mtb[0]])
                TT("dve", m1, sc.f32(gtb[2]), banks[5].ap, ALU.mult, [gtb[2], banks[5], mtb[1]], [mtb[1]])
                TT("pool", mgT[:, nb, :], m0, m1, ALU.add, [mtb[0], mtb[1]], [mgb])
            for b in gtb + mtb:
                sc.free(b)
            sc.free(ygmb)
            sc.free(ydTb)
            sc.free(ymTb)
            tr.tag = "wo"
            load_gb(1)
            for pair in range(2):
                pb = PBS[pair]
                for b in range(4):
                    s = wload(f"wo{b}")
                    wv = wview(s, 2048, "p (k n) -> p k n", k=2)
                    for kcl in range(2):
                        kc = 2 * b + kcl
                        for cl in range(2):
                            c = 2 * pair + cl
                            for hf in range(2):
                                MM(pb[cl * 2 + hf].ap, mgT[:, kc, c * P:(c + 1) * P], wv[:, kcl, hf * 512:(hf + 1) * 512],
                                   kc == 0, kc == KC - 1, [slotb[s], mgb], [pb[cl * 2 + hf]])
            sc.free(mgb)
            ln_stage("ln2", 1)

        tr.tag = "xT"
        xs = stage_x_load(0)
        stage_x_transpose(xs)
        pending = None
        for tile in range(nt):
            inter = None
            if pending is not None:
                pt_ = pending
                ln3_chunk(pt_, 0)
                ln3_chunk(pt_, 1)
                inter = {1: (lambda pt_=pt_: ln3_chunk(pt_, 2)), 2: (lambda pt_=pt_: ln3_chunk(pt_, 3))}
            hTb = ffn_s1("f1", inter)
            load_xres(tile)
            Cb, Sb = rope_tables(tile)
            load_gb(0)
            ffn_s3("f1", hTb)
            if dbg_d is not None and tile == 0:
                DUMP("x1", xres[:, :, :], xresb)
            mixer(tile, Cb, Sb)
            if dbg_d is not None and tile == 0:
                DUMP("x2", xres[:, :, :], xresb)
            xs_box = {}
            inter = None
            if tile + 1 < nt:
                inter = {12: (lambda t2=tile + 1: xs_box.__setitem__("xs", stage_x_load(t2)))}
            hTb = ffn_s1("f2", inter)
            if tile + 1 < nt:
                tr.tag = "xT"
                stage_x_transpose(xs_box["xs"])
            load_gb(2)
            ffn_s3("f2", hTb)
            pending = tile
        for c in range(4):
            ln3_chunk(pending, c)

        tr.add("sp", None, extra=out_events)
        if dbg_d is not None:
            tr.add("sp", None, extra=[("d", "dbg", tr.dma_cnt.get("dbg", 0))] if tr.dma_cnt.get("dbg", 0) else [])

        if SCHEDULE:
            tr.schedule()
        tr.resolve()

        with nc.Block() as block:
            @block.tensor
            def _(e):
                tr.emit("pe", e, sem_c, sem_d)

            @block.scalar
            def _(e):
                tr.emit("act", e, sem_c, sem_d)

            @block.vector
            def _(e):
                tr.emit("dve", e, sem_c, sem_d)

            @block.gpsimd
            def _(e):
                tr.emit("pool", e, sem_c, sem_d)

            @block.sync
            def _(e):
                tr.emit("sp", e, sem_c, sem_d)

    nc._tr = tr
    return nc


def host_inputs(inp):
    f32 = np.float32
    wflat = pack_weights(inp)
    ln_gb = np.stack([inp["ln1_g"][0], inp["ln1_b"][0], inp["ln2_g"][0], inp["ln2_b"][0],
                      inp["ln3_g"][0], inp["ln3_b"][0]]).astype(f32)
    gate_bT = np.ascontiguousarray(inp["gate_b"][0].reshape(24, P).T).astype(f32)
    gm_ln = np.concatenate([inp["gm_ln_g"][0], inp["gm_ln_b"][0]]).reshape(1, 1024).astype(f32)
    gm_wsT = np.ascontiguousarray(inp["gm_w_s"][0].transpose(2, 0, 1)).reshape(P, 512).astype(f32)
    gm_bs = inp["gm_b_s"][0].reshape(1, 512).astype(f32)
    lam = np.concatenate([inp["lambda_q1"][0], inp["lambda_k1"][0], inp["lambda_q2"][0],
                          inp["lambda_k2"][0]]).reshape(1, 256).astype(f32)
    dng = inp["diff_norm_g"][0].reshape(1, 128).astype(f32)
    consts = np.zeros((P, 258), dtype=f32)
    consts[:, 0:128] = np.eye(P, dtype=f32)
    consts[:, 128:256] = np.triu(np.ones((P, P), dtype=f32))
    inv_freq = ROPE_THETA ** (-np.arange(0, 16, 2, dtype=np.float64) / 16.0)
    d = np.arange(P) % 64
    consts[:, 256] = np.where(d < 16, inv_freq[d % 8], 0.0).astype(f32)
    consts[:, 257] = np.where(d < 8, -1.0, np.where(d < 16, 1.0, 0.0)).astype(f32)
    bc = lambda a: np.ascontiguousarray(np.broadcast_to(a.reshape(1, -1), (P, a.size))).astype(f32)
    ln_bc = np.stack([np.concatenate([bc(ln_gb[2 * l]), bc(ln_gb[2 * l + 1])], axis=1) for l in range(3)])
    ln_T = np.zeros((P, 48), dtype=f32)
    for l in range(3):
        ln_T[:, l * 16:l * 16 + 8] = ln_gb[2 * l].reshape(8, P).T
        ln_T[:, l * 16 + 8:l * 16 + 16] = ln_gb[2 * l + 1].reshape(8, P).T
    shared = dict(wflat=wflat, ln_gb=np.ascontiguousarray(ln_bc), ln_T=ln_T, gate_bT=gate_bT, gm_ln=bc(gm_ln),
                  gm_wsT=gm_wsT, gm_bs=bc(gm_bs), lam=bc(lam), dng=bc(dng), consts=consts)
    return shared


def kernel(**inputs):
    inp = {k: np.asarray(v) for k, v in inputs.items()}
    shared = host_inputs(inp)
    x = np.ascontiguousarray(inp["x"], dtype=np.float32)
    mem = np.ascontiguousarray(inp["mem"], dtype=np.float32)
    pos = np.ascontiguousarray(inp["positions"], dtype=np.int32)
    nc = build_program()
    in_maps = []
    for b in range(8):
        m = dict(shared)
        m["x"] = x[b]
        m["mem"] = mem[b]
        m["pos"] = np.ascontiguousarray(np.broadcast_to(pos[b].reshape(1, S), (P, S)))
        in_maps.append(m)
    res = run_bass_kernel_spmd(nc, in_maps, core_ids=list(range(8)))
    out = np.stack([np.asarray(res.results[b]["out"], dtype=np.float32).reshape(S, D) for b in range(8)], axis=0)
    return out
```
